# Optimizing a Trainium2 kernel written in Bass

```python
import jax, jax.numpy as jnp
from jax import lax
import numpy as np

D_MODEL = 1024
BATCH = 32
SEQ = 2048
DEPTH = 1
DEC_BATCH = 16
DEC_SEQ = 32
PAST_LEN = 1024

CHUNK = 64
N_HEADS = 16
HEAD_DIM = 64
ATTN_WIDTH = N_HEADS * HEAD_DIM
CONV_WIDTH = D_MODEL
CONV_K = 31
CONV_HIST = CONV_K - 1
Q_BLOCK = 128
EPS = 1e-6

OFF_Q = 0
OFF_K = OFF_Q + ATTN_WIDTH
OFF_V = OFF_K + ATTN_WIDTH
OFF_F = OFF_V + ATTN_WIDTH
OFF_GA = OFF_F + N_HEADS
OFF_UA = OFF_GA + ATTN_WIDTH
OFF_UB = OFF_UA + CONV_WIDTH
OFF_GB = OFF_UB + CONV_WIDTH
OFF_MA = OFF_GB + CONV_WIDTH
OFF_MB = OFF_MA + D_MODEL
IN_WIDTH = OFF_MB + D_MODEL

kernel_name = "fox_conformer_gated_hybrid_step"


def rmsnorm(x, g):
    xf = x.astype(jnp.float32)
    y = xf * lax.rsqrt(jnp.mean(xf * xf, axis=-1, keepdims=True) + EPS) * g.astype(jnp.float32)
    return y.astype(x.dtype)


def layernorm(x, g, b):
    xf = x.astype(jnp.float32)
    mu = jnp.mean(xf, axis=-1, keepdims=True)
    xc = xf - mu
    y = xc * lax.rsqrt(jnp.mean(xc * xc, axis=-1, keepdims=True) + EPS) * g.astype(jnp.float32) + b.astype(jnp.float32)
    return y.astype(x.dtype)


def mixer_inputs(x, norm_g, w_in, b_f, q_g, k_g):
    B, T, _ = x.shape
    h = rmsnorm(x, norm_g)
    cols = lambda a, w: h @ w_in[:, a:a + w]
    q = rmsnorm(cols(OFF_Q, ATTN_WIDTH).reshape(B, T, N_HEADS, HEAD_DIM), q_g) * (HEAD_DIM ** -0.5)
    k = rmsnorm(cols(OFF_K, ATTN_WIDTH).reshape(B, T, N_HEADS, HEAD_DIM), k_g)
    v = cols(OFF_V, ATTN_WIDTH).reshape(B, T, N_HEADS, HEAD_DIM)
    logf = jax.nn.log_sigmoid((cols(OFF_F, N_HEADS) + b_f).astype(jnp.float32))
    ga = cols(OFF_GA, ATTN_WIDTH)
    u = cols(OFF_UA, CONV_WIDTH) * jax.nn.sigmoid(cols(OFF_UB, CONV_WIDTH))
    gb = cols(OFF_GB, CONV_WIDTH)
    ma = cols(OFF_MA, D_MODEL)
    mb = cols(OFF_MB, D_MODEL)
    return q, k, v, logf, ga, u, gb, ma, mb


def fox_attend(q, k, v, cq, ck, qpos, kpos):
    s = jnp.einsum('bqhd,bkhd->bhqk', q.astype(jnp.float32), k.astype(jnp.float32))
    bias = jnp.transpose(cq, (0, 2, 1))[:, :, :, None] - jnp.transpose(ck, (0, 2, 1))[:, :, None, :]
    mask = qpos[:, None] >= kpos[None, :]
    s = jnp.where(mask[None, None], s + bias, jnp.finfo(jnp.float32).min)
    p = jax.nn.softmax(s, axis=-1)
    o = jnp.einsum('bhqk,bkhd->bqhd', p, v.astype(jnp.float32))
    return o.astype(v.dtype)


def causal_dwconv(u_ext, w_dw, b_dw):
    y = lax.conv_general_dilated(u_ext, w_dw[:, None, :], window_strides=(1,), padding='VALID',
                                 dimension_numbers=('NWC', 'WIO', 'NWC'), feature_group_count=CONV_WIDTH)
    return y + b_dw


def mixer_outputs(x, o_attn, conv, ga, gb, ma, mb, ln_g, ln_b, w_pa, w_pb, w_out):
    B, T, _ = x.shape
    ya = (o_attn.reshape(B, T, ATTN_WIDTH) * jax.nn.silu(ga)) @ w_pa
    yb = (jax.nn.silu(layernorm(conv, ln_g, ln_b)) * jax.nn.silu(gb)) @ w_pb
    m = jax.nn.sigmoid(ma) * ya + jax.nn.sigmoid(mb) * yb
    return x + m @ w_out


def setup_inputs(seed: int = 0) -> dict:
    key = jax.random.key(seed)
    ks = jax.random.split(key, 20)
    nrm = lambda k, shape, scale: jax.random.normal(k, shape, jnp.float32) * scale
    return {
        "x_prompt": nrm(ks[0], (BATCH, SEQ, D_MODEL), 1.0),
        "x_sample": nrm(ks[1], (DEC_BATCH, DEC_SEQ, D_MODEL), 1.0),
        "cache_k": nrm(ks[2], (DEPTH, DEC_BATCH, PAST_LEN, N_HEADS, HEAD_DIM), 1.0),
        "cache_v": nrm(ks[3], (DEPTH, DEC_BATCH, PAST_LEN, N_HEADS, HEAD_DIM), 1.0),
        "cache_logf": jax.nn.log_sigmoid(nrm(ks[4], (DEPTH, DEC_BATCH, PAST_LEN, N_HEADS), 1.0) + 2.5),
        "state_conv": nrm(ks[5], (DEPTH, DEC_BATCH, CONV_HIST, CONV_WIDTH), 0.5),
        "norm_g": 1.0 + nrm(ks[6], (DEPTH, D_MODEL), 0.02),
        "w_in": nrm(ks[7], (DEPTH, D_MODEL, IN_WIDTH), D_MODEL ** -0.5),
        "b_f": jax.random.uniform(ks[8], (DEPTH, N_HEADS), jnp.float32, 1.0, 4.0),
        "q_g": 1.0 + nrm(ks[9], (DEPTH, HEAD_DIM), 0.02),
        "k_g": 1.0 + nrm(ks[10], (DEPTH, HEAD_DIM), 0.02),
        "w_dw": nrm(ks[11], (DEPTH, CONV_K, CONV_WIDTH), CONV_K ** -0.5),
        "b_dw": nrm(ks[12], (DEPTH, CONV_WIDTH), 0.02),
        "ln_g": 1.0 + nrm(ks[13], (DEPTH, CONV_WIDTH), 0.02),
        "ln_b": nrm(ks[14], (DEPTH, CONV_WIDTH), 0.02),
        "w_pa": nrm(ks[15], (DEPTH, ATTN_WIDTH, D_MODEL), ATTN_WIDTH ** -0.5),
        "w_pb": nrm(ks[16], (DEPTH, CONV_WIDTH, D_MODEL), CONV_WIDTH ** -0.5),
        "w_out": nrm(ks[17], (DEPTH, D_MODEL, D_MODEL), D_MODEL ** -0.5),
    }


def reference(x_prompt, x_sample, cache_k, cache_v, cache_logf, state_conv, norm_g, w_in, b_f, q_g, k_g,
              w_dw, b_dw, ln_g, ln_b, w_pa, w_pb, w_out):
    xp, xs = x_prompt, x_sample
    kp_l, vp_l, fp_l, cp_l, ks_l, vs_l, fs_l, cs_l = [], [], [], [], [], [], [], []
    for l in range(DEPTH):
        q, k, v, lf, ga, u, gb, ma, mb = mixer_inputs(xp, norm_g[l], w_in[l], b_f[l], q_g[l], k_g[l])
        T = xp.shape[1]
        c = jnp.cumsum(lf, axis=1)
        blocks = []
        for i in range(T // Q_BLOCK):
            lo, hi = i * Q_BLOCK, (i + 1) * Q_BLOCK
            blocks.append(fox_attend(q[:, lo:hi], k[:, :hi], v[:, :hi], c[:, lo:hi], c[:, :hi],
                                     jnp.arange(lo, hi), jnp.arange(hi)))
        o = jnp.concatenate(blocks, axis=1)
        u_ext = jnp.pad(u, ((0, 0), (CONV_HIST, 0), (0, 0)))
        conv = causal_dwconv(u_ext, w_dw[l], b_dw[l])
        xp = mixer_outputs(xp, o, conv, ga, gb, ma, mb, ln_g[l], ln_b[l], w_pa[l], w_pb[l], w_out[l])
        kp_l.append(k); vp_l.append(v); fp_l.append(lf); cp_l.append(u_ext[:, -CONV_HIST:])

        qs, kn, vn, lfn, gas, us, gbs, mas, mbs = mixer_inputs(xs, norm_g[l], w_in[l], b_f[l], q_g[l], k_g[l])
        P = cache_k.shape[2]
        Ts = xs.shape[1]
        k_all = jnp.concatenate([cache_k[l], kn], axis=1)
        v_all = jnp.concatenate([cache_v[l], vn], axis=1)
        lf_all = jnp.concatenate([cache_logf[l].astype(jnp.float32), lfn], axis=1)
        c_all = jnp.cumsum(lf_all, axis=1)
        os_ = fox_attend(qs, k_all, v_all, c_all[:, P:], c_all, jnp.arange(P, P + Ts), jnp.arange(P + Ts))
        us_ext = jnp.concatenate([state_conv[l].astype(us.dtype), us], axis=1)
        convs = causal_dwconv(us_ext, w_dw[l], b_dw[l])
        xs = mixer_outputs(xs, os_, convs, gas, gbs, mas, mbs, ln_g[l], ln_b[l], w_pa[l], w_pb[l], w_out[l])
        ks_l.append(kn); vs_l.append(vn); fs_l.append(lfn); cs_l.append(us_ext[:, -CONV_HIST:])

    return (xp, xs,
            jnp.stack(kp_l), jnp.stack(vp_l), jnp.stack(fp_l), jnp.stack(cp_l),
            jnp.stack(ks_l), jnp.stack(vs_l), jnp.stack(fs_l), jnp.stack(cs_l))
```

```python
import numpy as np
from contextlib import ExitStack
import concourse.bass as bass
import concourse.mybir as mybir
from concourse.bass_utils import run_bass_kernel_spmd

F32 = mybir.dt.float32
BF16 = mybir.dt.bfloat16
AF = mybir.ActivationFunctionType
ALU = mybir.AluOpType
AX = mybir.AxisListType.X

D = 1024
H = 16
HD = 64
CONV_K = 31
HIST = 30
EPS = 1e-6
PAST = 1024
SSEQ = 32
OFF_Q = 0
OFF_K = 1024
OFF_V = 2048
OFF_F = 3072
OFF_GA = 3088
OFF_UA = OFF_GA + 1024
OFF_UB = OFF_UA + 1024
OFF_GB = OFF_UB + 1024
OFF_MA = OFF_GB + 1024
OFF_MB = OFF_MA + 1024
IN_W = OFF_MB + 1024

ENGS = ("pe", "act", "dve", "pool", "sp")


class Res:
    __slots__ = ("w", "rs")

    def __init__(self):
        self.w = None
        self.rs = []


class Op:
    __slots__ = ("eng", "fn", "deps", "sig", "sigval", "key", "ninc")

    def __init__(self, eng, fn, key):
        self.eng = eng
        self.fn = fn
        self.deps = ()
        self.sig = False
        self.sigval = 0
        self.key = key
        self.ninc = 1


class Sched:
    def __init__(self):
        self.ops = {e: [] for e in ENGS}
        self.fence = None
        self.fence_seen = {}
        self.last_dma = {}

    def add(self, eng, fn, reads=(), writes=(), key=None, ninc=1):
        op = Op(eng, fn, key)
        op.ninc = ninc
        deps = set()
        for r in reads:
            if r.w is not None:
                deps.add(r.w)
        for w in writes:
            if w.w is not None:
                deps.add(w.w)
            deps.update(w.rs)
        if self.fence is not None and self.fence_seen.get(eng) is not self.fence:
            deps.update(self.fence)
            self.fence_seen[eng] = self.fence
        if key is not None:
            prev = self.last_dma.get(key)
            if prev is not None:
                deps.add(prev)
            self.last_dma[key] = op
            op.sig = True
        if eng == "pe":
            deps = {d for d in deps if not (d.eng == "pe" and d.key is None)}
        for d in deps:
            d.sig = True
        op.deps = deps
        for w in writes:
            w.w = op
            w.rs = []
        for r in reads:
            r.rs.append(op)
        self.ops[eng].append(op)
        return op

    def barrier(self):
        f = set()
        for e in ENGS:
            for op in reversed(self.ops[e]):
                if op.key is None:
                    f.add(op)
                    break
        f.update(self.last_dma.values())
        self.fence = f
        self.fence_seen = {}


def build(NP, NS, TP):
    nc = bass.Bass("TRN2", target_bir_lowering=False)
    S = Sched()
    NTP = TP // 128
    NTMAX = max(NTP, PAST // 128 + 1)
    TKMAX = NTMAX * 128

    def din(name, shape, dt=F32):
        return nc.dram_tensor(name, list(shape), dt, kind="ExternalInput").ap()

    def dout(name, shape, dt=F32):
        return nc.dram_tensor(name, list(shape), dt, kind="ExternalOutput").ap()

    xp = din("xp", [NP, TP, D])
    xs = din("xs", [NS, SSEQ, D])
    ck = din("ck", [NS, PAST, D])
    cv = din("cv", [NS, PAST, D])
    cl = din("cl", [NS, PAST, H])
    sc = din("sc", [NS, HIST, D])
    w_in = din("w_in", [D, IN_W])
    w_pa = din("w_pa", [D, D])
    w_pb = din("w_pb", [D, D])
    w_out = din("w_out", [D, D])
    g8_d = din("g8", [128, 8])
    bf_d = din("bf_rep", [128, H])
    gq_d = din("gq_col", [64, 1])
    gk_d = din("gk_rep", [128, 128])
    wdw_d = din("wdw", [128, 8, CONV_K])
    bdw_d = din("bdw8", [128, 8])
    lng_d = din("lng8", [128, 8])
    lnb_d = din("lnb8", [128, 8])

    yp = dout("yp", [NP, TP, D])
    ys = dout("ys", [NS, SSEQ, D])
    kpo = dout("kp", [NP, TP, D])
    vpo = dout("vp", [NP, TP, D])
    fpo = dout("fp", [NP, TP, H])
    cpo = dout("cp", [NP, HIST, D])
    kso = dout("ks", [NS, SSEQ, D])
    vso = dout("vs", [NS, SSEQ, D])
    fso = dout("fs", [NS, SSEQ, H])
    cso = dout("cs", [NS, HIST, D])

    blocks = {}
    off = 0

    def addblk(name, W, ncols=None):
        nonlocal off
        n = ncols if ncols is not None else 8 * W
        blocks[name] = (off, W, n)
        off += n

    for p in range(8):
        addblk(f"qkv{p}", 384)
    addblk("f", 16)
    for p in range(8):
        addblk(f"ga{p}", 128)
    for c in range(8):
        addblk(f"uab{c}", 256)
    for c in range(8):
        addblk(f"gb{c}", 128)
    for c in range(8):
        addblk(f"pm{c}", 512)
    addblk("wo0", 512)
    addblk("wo1", 512)
    SCR_COLS = off
    scr = nc.dram_tensor("wscr", [128, SCR_COLS], BF16, kind="Internal").ap()

    st = ExitStack()
    with st:
        def sb(name, shape, dt=F32):
            return st.enter_context(nc.sbuf_tensor(name, list(shape), dt))

        ident_f = sb("ident_f", [128, 128])
        ident_b = sb("ident_b", [128, 128], BF16)
        mask_f = sb("mask_f", [128, 128])
        mask_b = sb("mask_b", [128, 128], BF16)
        ones_f = sb("ones_f", [128, 128])
        ones_b = sb("ones_b", [128, 128], BF16)
        nmask_b = sb("nmask_b", [128, 128], BF16)
        g8 = sb("g8s", [128, 8])
        bf_rep = sb("bf_reps", [128, H])
        gk_rep = sb("gk_reps", [128, 128])
        kscale = sb("kscale", [128, 1])
        wdw = sb("wdws", [128, 8, CONV_K])
        bdw8 = sb("bdw8s", [128, 8])
        lng8 = sb("lng8s", [128, 8])
        lnb8 = sb("lnb8s", [128, 8])
        nlnb8 = sb("nlnb8", [128, 8])
        negh = sb("negh", [128, 8])
        negh512 = sb("negh512", [128, 512])
        hT = sb("hT", [128, 8, TP], BF16)
        og = sb("og", [128, 8, TP], BF16)
        lf = sb("lf", [128, NTMAX, H])
        negC = sb("negC", [128, NTMAX, H])
        Cp = sb("Cp", [128, NTMAX, H, 3], BF16)
        ARENA_W = 33280
        arena = sb("arena", [128, ARENA_W])
        banks = [st.enter_context(nc.psum_tensor(f"bank{i}", [128, 512], F32)) for i in range(8)]
        RB = [Res() for _ in range(8)]
        sems = {e: st.enter_context(nc.semaphore(f"sem_{e}")) for e in ("pe", "act", "dve", "pool")}
        dma_sems = {}

        def dsem(key):
            if key not in dma_sems:
                dma_sems[key] = st.enter_context(nc.semaphore(f"dq_{key}"))
            return dma_sems[key]

        R_const = Res()
        R_scr = Res()
        R_hT = Res()
        R_og = Res()
        R_lf = Res()
        R_negC = Res()
        R_Cp = Res()

        class Carver:
            def __init__(self):
                self.off = 0

            def reset(self):
                self.off = 0

            def get(self, shape, dt=F32):
                n = int(np.prod(shape[1:]))
                words = n if dt == F32 else (n + 1) // 2
                a = arena[:, self.off:self.off + words]
                self.off += words
                assert self.off <= ARENA_W, ("arena overflow", self.off)
                if dt != F32:
                    a = a.bitcast(dt)[:, 0:n]
                if len(shape) == 3:
                    a = a.rearrange("p (a b) -> p a b", a=shape[1], b=shape[2])
                elif len(shape) == 4:
                    a = a.rearrange("p (a b c) -> p a b c", a=shape[1], b=shape[2], c=shape[3])
                return a

        CV = Carver()

        def bbf(i):
            return banks[i][:].bitcast(BF16)

        def A_(out, in_, func, reads, writes, scale=1.0, bias=None):
            def fn(e):
                if bias is None:
                    return e.activation(out=out, in_=in_, func=func, scale=scale)
                return e.activation(out=out, in_=in_, func=func, scale=scale, bias=bias)
            return S.add("act", fn, reads, writes)

        def TT(eng, out, in0, in1, op, reads, writes):
            return S.add(eng, lambda e: e.tensor_tensor(out=out, in0=in0, in1=in1, op=op), reads, writes)

        def TS(eng, out, in0, s1, op0, reads, writes, s2=None, op1=None):
            if op1 is None:
                return S.add(eng, lambda e: e.tensor_scalar(out=out, in0=in0, scalar1=s1, scalar2=None, op0=op0), reads, writes)
            return S.add(eng, lambda e: e.tensor_scalar(out=out, in0=in0, scalar1=s1, scalar2=s2, op0=op0, op1=op1), reads, writes)

        def STT(out, in0, scalar, in1, op0, op1, reads, writes):
            return S.add("dve", lambda e: e.scalar_tensor_tensor(out=out, in0=in0, scalar=scalar, in1=in1, op0=op0, op1=op1), reads, writes)

        def CP(eng, out, in_, reads, writes):
            return S.add(eng, lambda e: e.tensor_copy(out=out, in_=in_), reads, writes)

        def MS(eng, ap, val, writes):
            return S.add(eng, lambda e: e.memset(ap, val), (), writes)

        def DMA(out, in_, reads, writes, key, slow=False, eng="sp"):
            def fn(q):
                if slow:
                    return q.dma_start(out=out, in_=in_, allow_slow_non_contiguous=True)
                return q.dma_start(out=out, in_=in_)
            return S.add(eng, fn, reads, writes, key=key)

        def DMA2(pairs, reads, writes, key, eng="sp"):
            def fn(q):
                return [q.dma_start(out=o, in_=i) for (o, i) in pairs]
            return S.add(eng, fn, reads, writes, key=key, ninc=len(pairs))

        PH = {"cur": "pro", "log": [], "n": 0}

        def phase(name):
            PH["log"].append((PH["cur"], PH["n"]))
            PH["cur"] = name

        def MM(groups, reads, writes):
            PH["n"] += len(groups)
            def fn(e):
                ins = None
                for (o, l, r, s0, s1) in groups:
                    ins = e.matmul(o, lhsT=l, rhs=r, start=s0, stop=s1)
                return ins
            return S.add("pe", fn, reads, writes)

        def TR(items, reads, writes):
            PH["n"] += len(items)

            def fn(e):
                ins = None
                for (o, i) in items:
                    ins = e.transpose(o, i, ident_b[:])
                return ins
            return S.add("pe", fn, reads, writes)

        def sigmoid_chain(buf, src, Rbuf, Rsrc, scale=-1.0, bias=None):
            A_(buf, src, AF.Exp, [Rsrc], [Rbuf], scale=scale, bias=bias)
            A_(buf, buf, AF.Ln, [Rbuf], [Rbuf], bias=1.0)
            A_(buf, buf, AF.Exp, [Rbuf], [Rbuf], scale=-1.0)

        MS("pool", ident_f[:], 1.0, [R_const])
        S.add("pool", lambda e: e.affine_select(out=ident_f[:], in_=ident_f[:], pattern=[[-1, 128]], compare_op=ALU.is_equal,
                                                fill=0.0, base=0, channel_multiplier=1), [R_const], [R_const])
        CP("pool", ident_b[:], ident_f[:], [R_const], [R_const])
        MS("pool", mask_f[:], 1.0, [R_const])
        S.add("pool", lambda e: e.affine_select(out=mask_f[:], in_=mask_f[:], pattern=[[1, 128]], compare_op=ALU.is_ge,
                                                fill=0.0, base=0, channel_multiplier=-1), [R_const], [R_const])
        CP("pool", mask_b[:], mask_f[:], [R_const], [R_const])
        MS("pool", ones_f[:], 1.0, [R_const])
        TS("pool", nmask_b[:], mask_f[:], 30000.0, ALU.mult, [R_const], [R_const], s2=-30000.0, op1=ALU.add)
        MS("pool", ones_b[:], 1.0, [R_const])
        MS("pool", negh[:], -0.5, [R_const])
        MS("pool", negh512[:], -0.5, [R_const])
        MS("pool", kscale[:], 1.0, [R_const])
        for (dst, src) in ((g8, g8_d), (bf_rep, bf_d), (gk_rep, gk_d), (bdw8, bdw_d), (lng8, lng_d), (lnb8, lnb_d)):
            DMA(dst[:], src[:, :], [], [R_const], "misc")
        DMA(wdw[:], wdw_d[:, :, :], [], [R_const], "misc")
        gq_t = sb("gq_t", [64, 1])
        DMA(gq_t[:], gq_d[:, :], [], [R_const], "misc")
        TS("pool", kscale[0:64, :], gq_t[:], float(HD ** -0.5), ALU.mult, [R_const], [R_const])
        TS("pool", nlnb8[:], lnb8[:], -1.0, ALU.mult, [R_const], [R_const])

        CV.reset()
        NSF = 4
        stg_f = [CV.get([128, 8, 512]) for _ in range(NSF)]
        stg_b = [CV.get([128, 4096], BF16) for _ in range(3)]
        R_sf = [Res() for _ in range(NSF)]
        R_sbb = [Res(), Res(), Res()]
        segcnt = [0]
        blkcnt = [0]

        def wsrc(m):
            return m.rearrange("(kc p) n -> p kc n", p=128)

        def prep_block(name, segs):
            o_, W, n = blocks[name]
            bs = blkcnt[0] % 3
            blkcnt[0] += 1
            o = 0
            for (src, c0, w, scaled) in segs:
                fs = segcnt[0] % NSF
                eng = "dve" if segcnt[0] % 3 != 2 else "pool"
                segcnt[0] += 1
                DMA(stg_f[fs][:, :, 0:w], wsrc(src)[:, :, c0:c0 + w], [], [R_sf[fs]], ("wB%d" % fs) if fs < 3 else "x0")
                outv = stg_b[bs][:, 0:8 * W].rearrange("p (a b) -> p a b", a=8, b=W)[:, :, o:o + w]
                if scaled:
                    TT(eng, outv, stg_f[fs][:, :, 0:w], g8[:].unsqueeze(2).broadcast_to([128, 8, w]), ALU.mult,
                       [R_sf[fs], R_const], [R_sbb[bs]])
                else:
                    CP(eng, outv, stg_f[fs][:, :, 0:w], [R_sf[fs]], [R_sbb[bs]])
                o += w
            DMA(scr[:, o_:o_ + n], stg_b[bs][:, 0:n], [R_sbb[bs]], [], ("wA%d" % bs) if bs < 2 else "x1", eng="act")

        for p in range(8):
            prep_block(f"qkv{p}", [(w_in, OFF_Q + 128 * p, 128, True), (w_in, OFF_K + 128 * p, 128, True), (w_in, OFF_V + 128 * p, 128, True)])
        prep_block("f", [(w_in, OFF_F, 16, True)])
        for p in range(8):
            prep_block(f"ga{p}", [(w_in, OFF_GA + 128 * p, 128, True)])
        for c in range(8):
            prep_block(f"uab{c}", [(w_in, OFF_UA + 128 * c, 128, True), (w_in, OFF_UB + 128 * c, 128, True)])
        for c in range(8):
            prep_block(f"gb{c}", [(w_in, OFF_GB + 128 * c, 128, True)])
        for c in range(8):
            prep_block(f"pm{c}", [(w_pa, 128 * c, 128, False), (w_pb, 128 * c, 128, False),
                                  (w_in, OFF_MA + 128 * c, 128, True), (w_in, OFF_MB + 128 * c, 128, True)])
        prep_block("wo0", [(w_out, 0, 512, False)])
        prep_block("wo1", [(w_out, 512, 512, False)])
        S.barrier()

        class Ring:
            def __init__(self, name, n, words):
                self.name = name
                self.n = n
                self.words = words
                self.bufs = None
                self.res = [Res() for _ in range(n)]
                self.cnt = 0
                self.plan = []
                self.issued = 0
                self.taken = 0
                self.slots = {}

            def carve(self):
                self.bufs = [CV.get([128, self.words], BF16) for _ in range(self.n)]

            def set_plan(self, names):
                self.plan = list(names)
                self.issued = 0
                self.taken = 0

            def _issue(self):
                i = self.issued
                o_, W, n = blocks[self.plan[i]]
                s = self.cnt % self.n
                self.cnt += 1
                DMA(self.bufs[s][:, 0:n], scr[:, o_:o_ + n], [], [self.res[s]], "misc" if self.name == "wF" else f"{self.name}{s}")
                self.slots[i] = (self.bufs[s][:, 0:n], self.res[s])
                self.issued += 1

            def next(self, k=1):
                i = self.taken
                while self.issued < min(len(self.plan), i + self.n):
                    self._issue()
                self.taken += k
                if k == 1:
                    return self.slots.pop(i)
                return [self.slots.pop(i + q) for q in range(k)]

        def w3(ap, a, b):
            return ap.rearrange("p (a b) -> p a b", a=a, b=b)

        import os as _os
        _kseq2 = _os.environ.get("KSEQ2", "")
        seqno = [-1]

        def run_sequence(kind, b):
            seqno[0] += 1
            isP = kind == "p"
            NTn = NTP if isP else 1
            NTp = 0 if isP else PAST // 128
            NT = NTn + NTp
            CW = 512 if isP else 128
            NCH = NTn * 128 // CW
            VR = 128 if isP else SSEQ
            xsrc = xp[b] if isP else xs[b]
            y_o = yp[b] if isP else ys[b]
            k_o = kpo[b] if isP else kso[b]
            v_o = vpo[b] if isP else vso[b]
            f_o = fpo[b] if isP else fso[b]
            c_o = cpo[b] if isP else cso[b]

            CV.reset()
            NXT = 4
            xt = [CV.get([128, D]) for _ in range(NXT)]
            R_xt = [Res() for _ in range(NXT)]
            sq1 = CV.get([128, D])
            R_sq1 = Res()
            ss1 = [CV.get([128, 2]) for _ in range(2)]
            R_ss1 = [Res(), Res()]
            xb = [CV.get([128, D], BF16) for _ in range(2)]
            R_xb = [Res(), Res()]
            wA = Ring("wA", 2, 8 * 384)
            wA.carve()
            wG = Ring("wG", 2, 8 * 128)
            wG.carve()
            wF = Ring("wF", 1, 8 * 16)
            wF.carve()
            wA.set_plan([f"qkv{p}" for p in range(8)])
            wG.set_plan([f"ga{p}" for p in range(8)])
            wF.set_plan(["f"])
            QTa = CV.get([128, 2, TKMAX], BF16)
            KTa = CV.get([128, 2, TKMAX], BF16)
            va = CV.get([128, NTMAX, 192], BF16)
            R_QT, R_KT, R_va = Res(), Res(), Res()
            va4 = va.rearrange("p t (a b) -> p t a b", a=3, b=64)
            NR = 4
            PPB = (0, 1, 7, 2)
            TRB = (4, 5, 6, 3)
            sq2 = [CV.get([128, 256]) for _ in range(NR)]
            ss2 = [CV.get([128, 4]) for _ in range(NR)]
            rs2 = [CV.get([128, 4]) for _ in range(NR)]
            kt = [CV.get([128, 128]) for _ in range(NR)]
            KVB = 4 if isP else 1
            NKV = 3
            koutB = [CV.get([128, KVB, 128]) for _ in range(NKV)]
            voutB = [CV.get([128, KVB, 128]) for _ in range(NKV)]
            R_koutB = [Res() for _ in range(NKV)]
            R_voutB = [Res() for _ in range(NKV)]
            kvb_cnt = [0]
            qa = [CV.get([128, 2, 68], BF16) for _ in range(NR)]
            ka = [CV.get([128, 2, 68], BF16) for _ in range(NR)]
            if not isP:
                kinA = CV.get([128, NTp, 128])
                vinA = CV.get([128, NTp, 128])
            R_kinA, R_vinA = Res(), Res()
            R_sq2 = [Res() for _ in range(NR)]
            R_ss2 = [Res() for _ in range(NR)]
            R_rs2 = [Res() for _ in range(NR)]
            R_kt = [Res() for _ in range(NR)]
            R_qa = [Res() for _ in range(NR)]
            R_qaC = [Res() for _ in range(NR)]
            R_ka = [Res() for _ in range(NR)]
            pfa = CV.get([128, 16 + NTMAX, H])
            pfb = CV.get([128, 16 + NTMAX, H])
            R_pfa, R_pfb = Res(), Res()
            NPR = 4
            Pb = [CV.get([128, 512], BF16) for _ in range(NPR)]
            R_P = [Res() for _ in range(NPR)]
            gt = [CV.get([128, 512]) for _ in range(2)]
            sg = [CV.get([128, 512]) for _ in range(2)]
            ldb = CV.get([128, 512])
            t2 = CV.get([128, 512])
            R_gt = [Res(), Res()]
            R_sg = [Res(), Res()]
            R_ld, R_t2 = Res(), Res()
            zz = CV.get([128, NTMAX * H])
            r1 = CV.get([128, NTMAX * H])
            R_zz, R_r1 = Res(), Res()

            MS("pool", va[:, :, 64:128], 1.0, [R_va])
            for s_ in range(NR):
                MS("pool", ka[s_][:, :, 64:68], 1.0, [R_ka[s_]])
            MS("pool", pfa[:, 0:16, :], 0.0, [R_pfa])
            MS("pool", pfb[:, 0:16, :], 0.0, [R_pfb])

            phase(kind + "A1")
            def a1_load(t):
                x_ = t % NXT
                if not isP:
                    MS("pool", xt[x_][:], 0.0, [R_xt[x_]])
                DMA(xt[x_][0:VR, :], xsrc[t * 128:t * 128 + VR, :], [], [R_xt[x_]], f"x{x_}")

            def a1_s1(t):
                s_ = t % 2
                x_ = t % NXT
                A_(sq1[:, :], xt[x_][:, :], AF.Square, [R_xt[x_]], [R_sq1])
                S.add("dve", (lambda o, i: (lambda e: e.tensor_reduce(out=o, in_=i, axis=AX, op=ALU.add)))(ss1[s_][:, 0:1], sq1[:, :]),
                      [R_sq1], [R_ss1[s_]])
                TS("pool", ss1[s_][:, 0:1], ss1[s_][:, 0:1], 1.0 / D, ALU.mult, [R_ss1[s_]], [R_ss1[s_]], s2=EPS, op1=ALU.add)
                TT("pool", ss1[s_][:, 1:2], ss1[s_][:, 0:1], negh[:, 0:1], ALU.pow, [R_ss1[s_], R_const], [R_ss1[s_]])

            def a1_s2(t):
                s_ = t % 2
                x_ = t % NXT
                TS("dve", xb[s_][:, :], xt[x_][:, :], ss1[s_][:, 1:2], ALU.mult, [R_xt[x_], R_ss1[s_]], [R_xb[s_]])
                if t + NXT < NTn:
                    a1_load(t + NXT)
                bk = 4 + s_
                trv = w3(bbf(bk), 8, 128)
                TR([(trv[:, kc, :], xb[s_][:, kc * 128:(kc + 1) * 128]) for kc in range(8)], [R_xb[s_], R_const], [RB[bk]])

            def a1_s3(t):
                s_ = t % 2
                bk = 4 + s_
                trv = w3(bbf(bk), 8, 128)
                CP("dve", hT[:, :, t * 128:(t + 1) * 128], trv, [RB[bk]], [R_hT])
                MM([(banks[2][:, t * 16:(t + 1) * 16], hT[:, kc, t * 128:(t + 1) * 128], wfv[:, kc, :], kc == 0, kc == 7) for kc in range(8)],
                   [R_hT, R_wf], [RB[2]])

            wf_ap, R_wf = wF.next()
            wfv = w3(wf_ap, 8, 16)
            for t_ in range(min(NXT, NTn)):
                a1_load(t_)
            a1_s1(0)
            for t in range(NTn):
                a1_s2(t)
                if t + 1 < NTn:
                    a1_s1(t + 1)
                a1_s3(t)

            if seqno[0] >= 1 and _kseq2 == "A1":
                S.barrier()
                return
            phase(kind + "A2")
            zv = w3(zz[:, 0:NTn * H], NTn, H)
            TT("dve", zv, w3(banks[2][:, 0:NTn * H], NTn, H), bf_rep[:].unsqueeze(1).broadcast_to([128, NTn, H]), ALU.add,
               [RB[2], R_const], [R_zz])
            A_(zz[:, 0:NTn * H], zz[:, 0:NTn * H], AF.Exp, [R_zz], [R_zz], scale=-1.0)
            A_(zz[:, 0:NTn * H], zz[:, 0:NTn * H], AF.Ln, [R_zz], [R_zz], bias=1.0)
            if not isP:
                for t_ in range(NTp):
                    DMA(lf[:, t_, :], cl[b, t_ * 128:(t_ + 1) * 128, :], [], [R_lf], "misc")
            TS("dve", lf[:, NTp:NT, :], zv, -1.0, ALU.mult, [R_zz], [R_lf])
            if isP:
                for t_ in range(NT):
                    DMA(f_o[t_ * 128:(t_ + 1) * 128, :], lf[:, t_, :], [R_lf], [], "ast", eng="act")
            else:
                DMA(f_o[:, :], lf[0:VR, NTp, :], [R_lf], [], "ast", eng="act")
            PAD = 16
            CP("dve", pfa[:, PAD:PAD + 1, :], lf[:, 0:1, :], [R_lf], [R_pfa])
            if NT > 1:
                TT("dve", pfa[:, PAD + 1:PAD + NT, :], lf[:, 1:NT, :], lf[:, 0:NT - 1, :], ALU.add, [R_lf], [R_pfa])
            cur, Rcur, nxt, Rnxt = pfa, R_pfa, pfb, R_pfb
            sh = 2
            while sh < NT:
                TT("dve", nxt[:, PAD:PAD + NT, :], cur[:, PAD:PAD + NT, :], cur[:, PAD - sh:PAD + NT - sh, :], ALU.add, [Rcur], [Rnxt])
                cur, Rcur, nxt, Rnxt = nxt, Rnxt, cur, Rcur
                sh *= 2
            MM([(banks[3][:, 0:NT * H], ones_f[:], cur[:, PAD - 1:PAD - 1 + NT, :], True, False),
                (banks[3][:, 0:NT * H], mask_f[:], lf[:, 0:NT, :], False, True)], [R_lf, Rcur, R_const], [RB[3]])
            cps = w3(banks[3][:, 0:NT * H], NT, H)
            TS("dve", negC[:, 0:NT, :], cps, -1.0, ALU.mult, [RB[3]], [R_negC])
            r1v = w3(r1[:, 0:NT * H], NT, H)
            CP("dve", Cp[:, 0:NT, :, 0], cps, [RB[3]], [R_Cp])
            TT("dve", r1v, cps, Cp[:, 0:NT, :, 0], ALU.subtract, [RB[3], R_Cp], [R_r1])
            CP("dve", Cp[:, 0:NT, :, 1], r1v, [R_r1], [R_Cp])
            TT("dve", r1v, r1v, Cp[:, 0:NT, :, 1], ALU.subtract, [R_r1, R_Cp], [R_r1])
            CP("dve", Cp[:, 0:NT, :, 2], r1v, [R_r1], [R_Cp])

            if seqno[0] >= 1 and _kseq2 == "A2":
                S.barrier()
                return
            kvcnt = [0]
            for p in range(8):
                wq_ap, R_wq = wA.next()
                wqv = w3(wq_ap, 8, 384)
                wg_ap, R_wg = wG.next()
                wgv = w3(wg_ap, 8, 128)
                if NTp:
                    DMA2([(kinA[:, :, :], ck[b, :, p * 128:(p + 1) * 128].rearrange("(n q) c -> q n c", q=128)),
                          (vinA[:, :, :], cv[b, :, p * 128:(p + 1) * 128].rearrange("(n q) c -> q n c", q=128))], [], [R_kinA, R_vinA], "kvi0")
                for t in range(NTp):
                    s_ = kvcnt[0] % NR
                    kvcnt[0] += 1
                    CP("dve", ka[s_][:, :, 0:64], w3(kinA[:, t, :], 2, 64), [R_kinA], [R_ka[s_]])
                    CP("pool", va[:, t, 0:64], vinA[:, t, 0:64], [R_vinA], [R_va])
                    CP("pool", va[:, t, 128:192], vinA[:, t, 64:128], [R_vinA], [R_va])
                    bk = TRB[s_]
                    trv = w3(bbf(bk)[:, 0:512], 4, 128)
                    TR([(trv[0:67, 2 + hh, :], ka[s_][:, hh, 0:67]) for hh in range(2)], [R_ka[s_], R_const], [RB[bk]])
                    A_(KTa[0:67, :, t * 128:(t + 1) * 128], trv[0:67, 2:4, :], AF.Identity, [RB[bk], R_const], [R_KT], scale=kscale[0:67, 0:1])

                def proj(t):
                    bk = PPB[t % NR]
                    MM([(banks[bk][:, 0:384], hT[:, kc, t * 128:(t + 1) * 128], wqv[:, kc, :], kc == 0, kc == 7) for kc in range(8)],
                       [R_hT, R_wq], [RB[bk]])

                def kvslot(t):
                    return (kvbase[0] + t // KVB) % NKV

                def kout_t(t):
                    return koutB[kvslot(t)][:, t % KVB, :]

                def vout_t(t):
                    return voutB[kvslot(t)][:, t % KVB, :]

                def epi1(t):
                    bk = PPB[t % NR]
                    s_ = t % NR
                    pp = banks[bk]
                    A_(sq2[s_][:, :], pp[:, 0:256], AF.Square, [RB[bk]], [R_sq2[s_]])
                    A_(vout_t(t), pp[:, 256:384], AF.Copy, [RB[bk]], [R_voutB[kvslot(t)]])
                    S.add("dve", (lambda o, i: (lambda e: e.tensor_reduce(out=o, in_=i, axis=AX, op=ALU.add)))(ss2[s_][:, :], w3(sq2[s_][:, :], 4, 64)),
                          [R_sq2[s_]], [R_ss2[s_]])
                    A_(rs2[s_][:, :], ss2[s_][:, :], AF.Ln, [R_ss2[s_]], [R_rs2[s_]], scale=1.0 / HD, bias=EPS)
                    A_(rs2[s_][:, :], rs2[s_][:, :], AF.Exp, [R_rs2[s_]], [R_rs2[s_]], scale=-0.5)

                def epi2(t):
                    bk = PPB[t % NR]
                    s_ = t % NR
                    T = NTp + t
                    pp = banks[bk]
                    TT("dve", qa[s_][:, :, 0:64], w3(pp[:, 0:128], 2, 64), rs2[s_][:, 0:2].unsqueeze(2).broadcast_to([128, 2, 64]), ALU.mult,
                       [RB[bk], R_rs2[s_]], [R_qa[s_]])
                    CP("pool", qa[s_][:, :, 64:67], Cp[:, T, 2 * p:2 * p + 2, :], [R_Cp], [R_qaC[s_]])
                    TT("dve", w3(kt[s_][:, :], 2, 64), w3(pp[:, 128:256], 2, 64), rs2[s_][:, 2:4].unsqueeze(2).broadcast_to([128, 2, 64]), ALU.mult,
                       [RB[bk], R_rs2[s_]], [R_kt[s_]])
                    ks_ = kvslot(t)
                    TT("dve", kout_t(t), kt[s_][:, :], gk_rep[:, :], ALU.mult, [R_kt[s_], R_const], [R_koutB[ks_]])
                    CP("dve", ka[s_][:, :, 0:64], w3(kout_t(t), 2, 64), [R_koutB[ks_]], [R_ka[s_]])
                    CP("pool", va4[:, T, 0:3:2, :], w3(vout_t(t), 2, 64), [R_voutB[ks_]], [R_va])
                    if t % KVB == KVB - 1:
                        t0_ = t - (KVB - 1)
                        if isP:
                            kdst = k_o[t0_ * 128:(t + 1) * 128, p * 128:(p + 1) * 128].rearrange("(n q) c -> q n c", q=128)
                            vdst = v_o[t0_ * 128:(t + 1) * 128, p * 128:(p + 1) * 128].rearrange("(n q) c -> q n c", q=128)
                            DMA2([(kdst, koutB[ks_][:, :, :]), (vdst, voutB[ks_][:, :, :])], [R_koutB[ks_], R_voutB[ks_]], [], f"kvo{ks_}", eng="act")
                        else:
                            DMA2([(k_o[0:VR, p * 128:(p + 1) * 128], koutB[ks_][0:VR, 0, :]),
                                  (v_o[0:VR, p * 128:(p + 1) * 128], voutB[ks_][0:VR, 0, :])], [R_koutB[ks_], R_voutB[ks_]], [], f"kvo{ks_}", eng="act")

                def trans(t):
                    s_ = t % NR
                    T = NTp + t
                    bk = TRB[s_]
                    trv = w3(bbf(bk)[:, 0:512], 4, 128)
                    items = [(trv[0:67, hh, :], qa[s_][:, hh, 0:67]) for hh in range(2)]
                    items += [(trv[0:67, 2 + hh, :], ka[s_][:, hh, 0:67]) for hh in range(2)]
                    TR(items, [R_qa[s_], R_qaC[s_], R_ka[s_], R_const], [RB[bk]])
                    A_(QTa[0:67, :, T * 128:(T + 1) * 128], trv[0:67, 0:2, :], AF.Copy, [RB[bk]], [R_QT])
                    A_(KTa[0:67, :, T * 128:(T + 1) * 128], trv[0:67, 2:4, :], AF.Identity, [RB[bk], R_const], [R_KT], scale=kscale[0:67, 0:1])

                phase(kind + "A3")
                kvbase = [kvb_cnt[0]]
                kvb_cnt[0] += (NTn + KVB - 1) // KVB
                for t in range(min(3, NTn)):
                    proj(t)
                for t in range(min(2, NTn)):
                    epi1(t)
                epi2(0)
                for t in range(NTn):
                    if t + 3 < NTn:
                        proj(t + 3)
                    if t + 2 < NTn:
                        epi1(t + 2)
                    if t + 1 < NTn:
                        epi2(t + 1)
                    trans(t)

                phase(kind + "A4")
                SB = (7, 0, 1)

                fillers = []

                def ga_ops(G, now=False):
                    cols = slice(G * CW, (G + 1) * CW)
                    g_ = G % 2
                    ops_ = []
                    for kc in range(8):
                        ops_.append((lambda kc=kc: MM([(banks[6][:, 0:CW], wgv[:, kc, :], hT[:, kc, cols], kc == 0, kc == 7)], [R_hT, R_wg], [RB[6]])))
                    ops_.append(lambda: A_(gt[g_][:, 0:CW], banks[6][:, 0:CW], AF.Exp, [RB[6]], [R_gt[g_]], scale=-1.0))
                    ops_.append(lambda: A_(gt[g_][:, 0:CW], gt[g_][:, 0:CW], AF.Ln, [R_gt[g_]], [R_gt[g_]], bias=1.0))
                    ops_.append(lambda: A_(gt[g_][:, 0:CW], gt[g_][:, 0:CW], AF.Exp, [R_gt[g_]], [R_gt[g_]], scale=-1.0))
                    ops_.append(lambda: TT("dve", sg[g_][:, 0:CW], banks[6][:, 0:CW], gt[g_][:, 0:CW], ALU.mult, [RB[6], R_gt[g_]], [R_sg[g_]]))
                    if now:
                        for f_ in ops_:
                            f_()
                    else:
                        fillers.extend(ops_)

                def norm_ops(G, now=False):
                    cols = slice(G * CW, (G + 1) * CW)
                    g_ = G % 2
                    oa, ob_ = (2, 3) if G % 2 == 0 else (4, 5)
                    ops_ = [
                        lambda: S.add("dve", lambda e: e.reciprocal(out=ldb[0:64, 0:CW], in_=banks[oa][64:128, 0:CW]), [RB[oa]], [R_ld]),
                        lambda: S.add("dve", lambda e: e.reciprocal(out=ldb[64:128, 0:CW], in_=banks[ob_][0:64, 0:CW]), [RB[ob_]], [R_ld]),
                        lambda: TT("dve", t2[:, 0:CW], sg[g_][:, 0:CW], ldb[:, 0:CW], ALU.mult, [R_sg[g_], R_ld], [R_t2]),
                        lambda: TT("dve", og[0:64, p, cols], banks[oa][0:64, 0:CW], t2[0:64, 0:CW], ALU.mult, [RB[oa], R_t2], [R_og]),
                        lambda: TT("dve", og[64:128, p, cols], banks[ob_][64:128, 0:CW], t2[64:128, 0:CW], ALU.mult, [RB[ob_], R_t2], [R_og]),
                    ]
                    if now:
                        for f_ in ops_:
                            f_()
                    else:
                        fillers.extend(ops_)

                blks = []
                for G in range(NCH):
                    first_diag = NTp + G * CW // 128
                    nkb = first_diag + CW // 128
                    for hh in range(2):
                        for j in range(nkb):
                            blks.append(dict(G=G, hh=hh, j=j, c0=max(0, j - first_diag) * 128, diag=j >= first_diag,
                                             ob=((2, 3) if G % 2 == 0 else (4, 5))[hh], hoff=0 if hh == 0 else 64,
                                             head=2 * p + hh, first=j == 0, last=j == nkb - 1, qbase=NTp * 128 + G * CW))

                def qk(i):
                    bl = blks[i]
                    sbk = SB[i % 3]
                    c0 = bl["c0"]
                    j = bl["j"]
                    hh = bl["hh"]
                    kT = KTa[0:67, hh, j * 128:(j + 1) * 128]
                    qb = bl["qbase"]
                    if bl["diag"]:
                        grp = [(banks[sbk][:, c0:c0 + 128], ident_b[:, :], nmask_b[:, :], True, False),
                               (banks[sbk][:, c0:c0 + 128], kT, QTa[0:67, hh, qb + c0:qb + c0 + 128], False, True)]
                        if c0 + 128 < CW:
                            grp.append((banks[sbk][:, c0 + 128:CW], kT, QTa[0:67, hh, qb + c0 + 128:qb + CW], True, True))
                        MM(grp, [R_KT, R_QT, R_const], [RB[sbk]])
                    else:
                        MM([(banks[sbk][:, c0:CW], kT, QTa[0:67, hh, qb + c0:qb + CW], True, True)], [R_KT, R_QT], [RB[sbk]])

                ga_ops(0, now=True)
                nb = len(blks)
                qk(0)
                if nb > 1:
                    qk(1)
                for i in range(nb):
                    bl = blks[i]
                    if i + 2 < nb:
                        qk(i + 2)
                    c0 = bl["c0"]
                    j = bl["j"]
                    sbk = SB[i % 3]
                    ps_ = i % NPR
                    A_(Pb[ps_][:, c0:CW], banks[sbk][:, c0:CW], AF.Exp, [RB[sbk], R_negC], [R_P[ps_]], bias=negC[:, j, bl["head"]:bl["head"] + 1])
                    MM([(banks[bl["ob"]][:, c0:CW], va[:, j, bl["hoff"]:bl["hoff"] + 128], Pb[ps_][:, c0:CW], bl["first"], bl["last"])],
                       [R_va, R_P[ps_]], [RB[bl["ob"]]])
                    if bl["hh"] == 0 and j == 1 and bl["G"] + 1 < NCH:
                        ga_ops(bl["G"] + 1)
                    if bl["hh"] == 1 and bl["last"]:
                        norm_ops(bl["G"])
                    if fillers:
                        fillers.pop(0)()
                while fillers:
                    fillers.pop(0)()

            S.barrier()
            if seqno[0] >= 1 and _kseq2 == "A":
                return
            CV.reset()
            wB = Ring("wB", 3, 4096)
            wB.carve()
            plan = []
            plan += [f"uab{c}" for c in range(8)]
            for _g in range(NCH):
                for c in range(8):
                    plan.append(f"gb{c}")
                    if _g + 1 < NCH:
                        plan.append(f"uab{c}")
                plan += [f"pm{c}" for c in range(8)] + ["wo0", "wo1"]
            wB.set_plan(plan)
            dgb = [CV.get([128, CONV_K, 128], BF16) for _ in range(2)]
            R_dgb = [Res(), Res()]
            u = CV.get([128, 8, HIST + 512], BF16)
            ufp = CV.get([128, 8, 32])
            conv = CV.get([128, 8, 512])
            cbf = [CV.get([128, 512], BF16) for _ in range(2)]
            csq = [CV.get([128, 512], BF16) for _ in range(2)]
            mean = CV.get([128, 512])
            msq = CV.get([128, 512])
            rstd = CV.get([128, 512])
            Bm = CV.get([128, 512])
            tb = [CV.get([128, 512]) for _ in range(2)]
            tb2 = [CV.get([128, 512]) for _ in range(2)]
            tq = CV.get([128, 512])
            v0 = CV.get([128, 512])
            sv = CV.get([128, 512])
            t3 = CV.get([128, 512])
            zb = CV.get([128, 8, 512], BF16)
            mT = CV.get([128, 8, 512], BF16)
            NXR = 4
            xr = [CV.get([128, D]) for _ in range(NXR)]
            tailT = CV.get([128, D])
            R_tail = Res()
            R_u, R_ufp, R_conv = Res(), Res(), Res()
            R_cbf = [Res(), Res()]
            R_csq = [Res(), Res()]
            R_mean, R_msq, R_rstd, R_Bm = Res(), Res(), Res(), Res()
            R_tb = [Res(), Res()]
            R_tb2 = [Res(), Res()]
            R_tq, R_v0, R_sv, R_t3, R_zb, R_m = Res(), Res(), Res(), Res(), Res(), Res()
            R_xr = [Res() for _ in range(NXR)]
            prebuilt = [False]
            tcnt = [0]
            xcnt = [0]

            def hist_for(G):
                if isP:
                    if G == 0:
                        MS("pool", u[:, :, 0:HIST], 0.0, [R_u])
                    else:
                        CP("pool", u[:, :, 0:HIST], u[:, :, CW:CW + HIST], [R_u], [R_u])
                else:
                    DMA(tailT[0:HIST, :], sc[b][:, :], [], [R_tail], "misc")
                    hv = banks[6][:, 0:256].rearrange("p (a b) -> p a b", a=8, b=32)
                    S.add("pe", (lambda hv_: (lambda e: [e.transpose(hv_[:, c_, 0:HIST], tailT[0:HIST, c_ * 128:(c_ + 1) * 128], ident_f[0:HIST, 0:HIST]) for c_ in range(8)][-1]))(hv),
                          [R_tail, R_const], [RB[6]])
                    TS("dve", u[:, :, 0:HIST], hv[:, :, 0:HIST], 2.0, ALU.mult, [RB[6]], [R_u])

            def b1_chunk(G, c):
                cols_ = slice(G * CW, (G + 1) * CW)
                last_ = G == NCH - 1
                wb_ap, R_wb = wB.next()
                wv = w3(wb_ap, 8, 256)
                ba = (2 * c) % 4
                bb_ = ba + 1
                MM([(banks[ba][:, 0:CW], wv[:, kc, 0:128], hT[:, kc, cols_], kc == 0, kc == 7) for kc in range(8)], [R_hT, R_wb], [RB[ba]])
                MM([(banks[bb_][:, 0:CW], wv[:, kc, 128:256], hT[:, kc, cols_], kc == 0, kc == 7) for kc in range(8)], [R_hT, R_wb], [RB[bb_]])
                ts_ = tcnt[0] % 2
                tcnt[0] += 1
                A_(tb[ts_][:, 0:CW], banks[bb_][:, 0:CW], AF.Tanh, [RB[bb_]], [R_tb[ts_]], scale=0.5)
                STT(u[:, c, HIST:HIST + CW], tb[ts_][:, 0:CW], 1.0, banks[ba][:, 0:CW], ALU.add, ALU.mult, [RB[ba], R_tb[ts_]], [R_u])
                if last_:
                    o32 = CW - 32 if isP else 0
                    STT(ufp[:, c, :], tb[ts_][:, o32:o32 + 32], 1.0, banks[ba][:, o32:o32 + 32], ALU.add, ALU.mult, [RB[ba], R_tb[ts_]], [R_ufp])

            def b1_tail(G):
                if G == NCH - 1:
                    S.add("pe", lambda e: [e.transpose(banks[2 + c_ // 4][0:32, (c_ % 4) * 128:(c_ % 4 + 1) * 128], ufp[:, c_, :], ident_f[:, :]) for c_ in range(8)][-1],
                          [R_ufp, R_const], [RB[2], RB[3]])
                    A_(tailT[0:32, 0:512], banks[2][0:32, :], AF.Identity, [RB[2]], [R_tail], scale=0.5)
                    A_(tailT[0:32, 512:1024], banks[3][0:32, :], AF.Identity, [RB[3]], [R_tail], scale=0.5)
                    DMA(c_o[:, :], tailT[2:32, :], [R_tail], [], "ast", eng="act")

            for G in range(NCH):
                cols = slice(G * CW, (G + 1) * CW)
                last = G == NCH - 1
                if G == 0:
                    hist_for(0)
                    phase(kind + "B1")
                    for c in range(8):
                        b1_chunk(0, c)
                    b1_tail(0)
                phase(kind + "B2")
                def stats_mm(c):
                    s_ = c % 2
                    MM([(banks[6][:, 0:CW], ones_b[:, :], cbf[s_][:, 0:CW], c == 0, c == 7)], [R_cbf[s_], R_const], [RB[6]])
                    MM([(banks[7][:, 0:CW], ones_b[:, :], csq[s_][:, 0:CW], c == 0, c == 7)], [R_csq[s_], R_const], [RB[7]])

                def mkdiag(c):
                    d_ = c % 2
                    TT("dve", dgb[d_][:, :, :], ident_b[:].unsqueeze(1).broadcast_to([128, CONV_K, 128]),
                       wdw[:, c, :].unsqueeze(2).broadcast_to([128, CONV_K, 128]), ALU.mult, [R_const], [R_dgb[d_]])

                if not prebuilt[0]:
                    mkdiag(0)
                for c in range(8):
                    d_ = c % 2
                    if c + 1 < 8 and not (c == 0 and prebuilt[0]):
                        mkdiag(c + 1)
                    wdv, R_wd = dgb[d_], R_dgb[d_]
                    bk = 4 + c % 2
                    MM([(banks[bk][:, 0:CW], wdv[:, j, :], u[:, c, j:j + CW], j == 0, j == CONV_K - 1) for j in range(CONV_K)], [R_u, R_wd], [RB[bk]])
                    if c > 0:
                        stats_mm(c - 1)
                    s_ = c % 2
                    A_(conv[:, c, 0:CW], banks[bk][:, 0:CW], AF.Identity, [RB[bk], R_const], [R_conv], scale=0.5, bias=bdw8[:, c:c + 1])
                    A_(csq[s_][:, 0:CW], banks[bk][:, 0:CW], AF.Square, [RB[bk], R_const], [R_csq[s_]], scale=0.5, bias=bdw8[:, c:c + 1])
                    CP("dve", cbf[s_][:, 0:CW], conv[:, c, 0:CW], [R_conv], [R_cbf[s_]])
                stats_mm(7)
                TS("dve", mean[:, 0:CW], banks[6][:, 0:CW], 1.0 / D, ALU.mult, [RB[6]], [R_mean])
                TT("dve", msq[:, 0:CW], mean[:, 0:CW], mean[:, 0:CW], ALU.mult, [R_mean], [R_msq])
                STT(msq[:, 0:CW], banks[7][:, 0:CW], 1.0 / D, msq[:, 0:CW], ALU.mult, ALU.subtract, [RB[7], R_msq], [R_msq])
                A_(rstd[:, 0:CW], msq[:, 0:CW], AF.Ln, [R_msq], [R_rstd], bias=EPS)
                A_(rstd[:, 0:CW], rstd[:, 0:CW], AF.Exp, [R_rstd], [R_rstd], scale=-0.5)
                STT(Bm[:, 0:CW], mean[:, 0:CW], -1.0, rstd[:, 0:CW], ALU.mult, ALU.mult, [R_mean, R_rstd], [R_Bm])
                if G + 1 < NCH:
                    hist_for(G + 1)
                phase(kind + "B3")
                for c in range(8):
                    wb_ap, R_wb = wB.next()
                    wv = w3(wb_ap, 8, 128)
                    bk = 6 + c % 2
                    MM([(banks[bk][:, 0:CW], wv[:, kc, :], hT[:, kc, cols], kc == 0, kc == 7) for kc in range(8)], [R_hT, R_wb], [RB[bk]])
                    STT(tq[:, 0:CW], conv[:, c, 0:CW], lng8[:, c:c + 1], rstd[:, 0:CW], ALU.mult, ALU.mult, [R_conv, R_rstd, R_const], [R_tq])
                    STT(v0[:, 0:CW], Bm[:, 0:CW], lng8[:, c:c + 1], tq[:, 0:CW], ALU.mult, ALU.add, [R_Bm, R_tq, R_const], [R_v0])
                    ts_ = tcnt[0] % 2
                    tcnt[0] += 1
                    TS("dve", v0[:, 0:CW], v0[:, 0:CW], lnb8[:, c:c + 1], ALU.add, [R_v0, R_const], [R_v0])
                    A_(tb[ts_][:, 0:CW], v0[:, 0:CW], AF.Tanh, [R_v0], [R_tb[ts_]], scale=0.5)
                    STT(sv[:, 0:CW], tb[ts_][:, 0:CW], 1.0, v0[:, 0:CW], ALU.add, ALU.mult, [R_v0, R_tb[ts_]], [R_sv])
                    A_(tb2[ts_][:, 0:CW], banks[bk][:, 0:CW], AF.Tanh, [RB[bk]], [R_tb2[ts_]], scale=0.5)
                    STT(t3[:, 0:CW], tb2[ts_][:, 0:CW], 1.0, banks[bk][:, 0:CW], ALU.add, ALU.mult, [RB[bk], R_tb2[ts_]], [R_t3])
                    STT(zb[:, c, 0:CW], t3[:, 0:CW], 0.25, sv[:, 0:CW], ALU.mult, ALU.mult, [R_t3, R_sv], [R_zb])
                    if G + 1 < NCH:
                        b1_chunk(G + 1, c)
                if G + 1 < NCH:
                    b1_tail(G + 1)
                phase(kind + "B4")
                for c in range(8):
                    wb_ap, R_wb = wB.next()
                    wv = w3(wb_ap, 8, 512)
                    base = 0 if c % 2 == 0 else 4
                    MM([(banks[base][:, 0:CW], wv[:, kc, 0:128], og[:, kc, cols], kc == 0, kc == 7) for kc in range(8)], [R_og, R_wb], [RB[base]])
                    MM([(banks[base + 1][:, 0:CW], wv[:, kc, 128:256], zb[:, kc, 0:CW], kc == 0, kc == 7) for kc in range(8)], [R_zb, R_wb], [RB[base + 1]])
                    MM([(banks[base + 2][:, 0:CW], wv[:, kc, 256:384], hT[:, kc, cols], kc == 0, kc == 7) for kc in range(8)], [R_hT, R_wb], [RB[base + 2]])
                    MM([(banks[base + 3][:, 0:CW], wv[:, kc, 384:512], hT[:, kc, cols], kc == 0, kc == 7) for kc in range(8)], [R_hT, R_wb], [RB[base + 3]])
                    ts_ = tcnt[0] % 2
                    tcnt[0] += 1
                    A_(tb[ts_][:, 0:CW], banks[base + 2][:, 0:CW], AF.Tanh, [RB[base + 2]], [R_tb[ts_]], scale=0.5)
                    A_(tb2[ts_][:, 0:CW], banks[base + 3][:, 0:CW], AF.Tanh, [RB[base + 3]], [R_tb2[ts_]], scale=0.5)
                    STT(tq[:, 0:CW], tb[ts_][:, 0:CW], 1.0, banks[base][:, 0:CW], ALU.add, ALU.mult, [RB[base], R_tb[ts_]], [R_tq])
                    STT(t3[:, 0:CW], tb2[ts_][:, 0:CW], 1.0, banks[base + 1][:, 0:CW], ALU.add, ALU.mult, [RB[base + 1], R_tb2[ts_]], [R_t3])
                    TT("dve", mT[:, c, 0:CW], tq[:, 0:CW], t3[:, 0:CW], ALU.add, [R_tq, R_t3], [R_m])
                prebuilt[0] = G + 1 < NCH
                phase(kind + "B5")
                wo = [(w3(a_, 8, 512), r_) for (a_, r_) in wB.next(2)]
                for tt in range(CW // 128):
                    xs_ = xcnt[0] % NXR
                    xcnt[0] += 1
                    trow = G * CW + tt * 128
                    DMA(xr[xs_][0:VR, :], xsrc[trow:trow + VR, :], [], [R_xr[xs_]], f"x{xs_}")
                    for hf in range(2):
                        bk = (2 * tt + hf) % 4
                        MM([(banks[bk][:, 0:512], mT[:, kc, tt * 128:(tt + 1) * 128], wo[hf][0][:, kc, :], kc == 0, kc == 7) for kc in range(8)],
                           [R_m, wo[hf][1]], [RB[bk]])
                        STT(xr[xs_][0:VR, hf * 512:(hf + 1) * 512], banks[bk][0:VR, 0:512], 0.5, xr[xs_][0:VR, hf * 512:(hf + 1) * 512], ALU.mult, ALU.add,
                            [RB[bk], R_xr[xs_]], [R_xr[xs_]])
                    DMA(y_o[trow:trow + VR, :], xr[xs_][0:VR, :], [R_xr[xs_]], [], f"yo{xs_ % 2}", eng="act")
                    if prebuilt[0] and tt == 0:
                        mkdiag(0)
                    if prebuilt[0] and tt == 2:
                        mkdiag(1)
            S.barrier()

        import os as _os
        _kstop = _os.environ.get("KSTOP", "")
        for b in range(NS):
            if _kstop not in ("pro", "ponly"):
                run_sequence("s", b)
        for b in range(NP):
            if _kstop not in ("pro", "sonly"):
                run_sequence("p", b)

        cnt = {e: 0 for e in ("pe", "act", "dve", "pool")}
        dcnt = {}
        for e in ENGS:
            for op in S.ops[e]:
                if op.key is not None:
                    dcnt[op.key] = dcnt.get(op.key, 0) + 16 * op.ninc
                    op.sigval = dcnt[op.key]
                    dsem(op.key)
                elif op.sig:
                    cnt[e] += 1
                    op.sigval = cnt[e]

        def emit(eng, h):
            waited = {}
            for op in S.ops[eng]:
                need = {}
                for d in op.deps:
                    k = ("d", d.key) if d.key is not None else ("e", d.eng)
                    if d.sigval > need.get(k, 0):
                        need[k] = d.sigval
                for k, v in need.items():
                    if waited.get(k, 0) < v:
                        sem = dma_sems[k[1]] if k[0] == "d" else sems[k[1]]
                        h.wait_ge(sem, v)
                        waited[k] = v
                ins = op.fn(h)
                if op.key is not None:
                    for i_ in (ins if isinstance(ins, (list, tuple)) else [ins]):
                        i_.then_inc(dma_sems[op.key], 16)
                elif op.sig:
                    ins.then_inc(sems[eng], 1)
            if eng == "sp":
                for k, v in dcnt.items():
                    h.wait_ge(dma_sems[k], v)

        global LAST_STATS
        LAST_STATS = {e: len(S.ops[e]) for e in ENGS}
        LAST_STATS["sems"] = len(dma_sems) + 4
        phase("end")
        LAST_STATS["phases"] = list(PH["log"])
        with nc.Block() as block:
            @block.tensor
            def _(t):
                emit("pe", t)

            @block.scalar
            def _(a):
                emit("act", a)

            @block.vector
            def _(v):
                emit("dve", v)

            @block.gpsimd
            def _(g):
                emit("pool", g)

            @block.sync
            def _(s):
                emit("sp", s)
    return nc


_NC_CACHE = {}
LAST_STATS = {}


def _get_nc(NP, NS, TP):
    k = (NP, NS, TP)
    if k not in _NC_CACHE:
        _NC_CACHE[k] = build(NP, NS, TP)
    return _NC_CACHE[k]


def _common_maps(norm_g, w_in, b_f, q_g, k_g, w_dw, b_dw, ln_g, ln_b, w_pa, w_pb, w_out):
    f = np.float32
    c = lambda a: np.ascontiguousarray(a, dtype=f)
    return {
        "w_in": c(w_in[0]), "w_pa": c(w_pa[0]), "w_pb": c(w_pb[0]), "w_out": c(w_out[0]),
        "g8": c(norm_g[0].reshape(8, 128).T),
        "bf_rep": c(np.tile(b_f[0][None, :], (128, 1))),
        "gq_col": c(q_g[0].reshape(64, 1)),
        "gk_rep": c(np.tile(k_g[0][None, :], (128, 2))),
        "wdw": c(w_dw[0].reshape(CONV_K, 8, 128).transpose(2, 1, 0)),
        "bdw8": c(b_dw[0].reshape(8, 128).T),
        "lng8": c(ln_g[0].reshape(8, 128).T),
        "lnb8": c(ln_b[0].reshape(8, 128).T),
    }


def kernel(x_prompt, x_sample, cache_k, cache_v, cache_logf, state_conv, norm_g, w_in, b_f, q_g, k_g,
           w_dw, b_dw, ln_g, ln_b, w_pa, w_pb, w_out):
    NCORE = 8
    x_prompt = np.asarray(x_prompt)
    x_sample = np.asarray(x_sample)
    B, T, _ = x_prompt.shape
    BS = x_sample.shape[0]
    NP = B // NCORE
    NS = BS // NCORE
    nc = _get_nc(NP, NS, T)
    common = _common_maps(*[np.asarray(a) for a in (norm_g, w_in, b_f, q_g, k_g, w_dw, b_dw, ln_g, ln_b, w_pa, w_pb, w_out)])
    ck = np.asarray(cache_k)[0].reshape(BS, PAST, D)
    cv = np.asarray(cache_v)[0].reshape(BS, PAST, D)
    cl = np.asarray(cache_logf)[0]
    sc = np.asarray(state_conv)[0]
    f = np.float32
    in_maps = []
    for i in range(NCORE):
        m = dict(common)
        m["xp"] = np.ascontiguousarray(x_prompt[i * NP:(i + 1) * NP], dtype=f)
        m["xs"] = np.ascontiguousarray(x_sample[i * NS:(i + 1) * NS], dtype=f)
        m["ck"] = np.ascontiguousarray(ck[i * NS:(i + 1) * NS], dtype=f)
        m["cv"] = np.ascontiguousarray(cv[i * NS:(i + 1) * NS], dtype=f)
        m["cl"] = np.ascontiguousarray(cl[i * NS:(i + 1) * NS], dtype=f)
        m["sc"] = np.ascontiguousarray(sc[i * NS:(i + 1) * NS], dtype=f)
        in_maps.append(m)
    res = run_bass_kernel_spmd(nc, in_maps, core_ids=list(range(NCORE)))
    R = res.results
    cat = lambda n: np.concatenate([np.asarray(r[n], dtype=f) for r in R], axis=0)
    y_p = cat("yp")
    y_s = cat("ys")
    k_p = cat("kp").reshape(1, B, T, H, HD)
    v_p = cat("vp").reshape(1, B, T, H, HD)
    f_p = cat("fp").reshape(1, B, T, H)
    c_p = cat("cp").reshape(1, B, HIST, D)
    k_s = cat("ks").reshape(1, BS, SSEQ, H, HD)
    v_s = cat("vs").reshape(1, BS, SSEQ, H, HD)
    f_s = cat("fs").reshape(1, BS, SSEQ, H)
    c_s = cat("cs").reshape(1, BS, HIST, D)
    return (y_p, y_s, k_p, v_p, f_p, c_p, k_s, v_s, f_s, c_s)
```

```python
import numpy as np
from contextlib import ExitStack
import concourse.bass as bass
import concourse.mybir as mybir
from concourse.bass_utils import run_bass_kernel_spmd

F32 = mybir.dt.float32
BF16 = mybir.dt.bfloat16
AF = mybir.ActivationFunctionType
ALU = mybir.AluOpType
AX = mybir.AxisListType.X

D = 1024
H = 16
HD = 64
CONV_K = 31
HIST = 30
EPS = 1e-6
PAST = 1024
SSEQ = 32
OFF_Q = 0
OFF_K = 1024
OFF_V = 2048
OFF_F = 3072
OFF_GA = 3088
OFF_UA = OFF_GA + 1024
OFF_UB = OFF_UA + 1024
OFF_GB = OFF_UB + 1024
OFF_MA = OFF_GB + 1024
OFF_MB = OFF_MA + 1024
IN_W = OFF_MB + 1024

ENGS = ("pe", "act", "dve", "pool", "sp")


class Res:
    __slots__ = ("w", "rs")

    def __init__(self):
        self.w = None
        self.rs = []


class Op:
    __slots__ = ("eng", "fn", "deps", "sig", "sigval", "key", "ninc")

    def __init__(self, eng, fn, key):
        self.eng = eng
        self.fn = fn
        self.deps = ()
        self.sig = False
        self.sigval = 0
        self.key = key
        self.ninc = 1


class Sched:
    def __init__(self):
        self.ops = {e: [] for e in ENGS}
        self.fence = None
        self.fence_seen = {}
        self.last_dma = {}

    def add(self, eng, fn, reads=(), writes=(), key=None, ninc=1):
        op = Op(eng, fn, key)
        op.ninc = ninc
        deps = set()
        for r in reads:
            if r.w is not None:
                deps.add(r.w)
        for w in writes:
            if w.w is not None:
                deps.add(w.w)
            deps.update(w.rs)
        if self.fence is not None and self.fence_seen.get(eng) is not self.fence:
            deps.update(self.fence)
            self.fence_seen[eng] = self.fence
        if key is not None:
            prev = self.last_dma.get(key)
            if prev is not None:
                deps.add(prev)
            self.last_dma[key] = op
            op.sig = True
        if eng == "pe":
            deps = {d for d in deps if not (d.eng == "pe" and d.key is None)}
        for d in deps:
            d.sig = True
        op.deps = deps
        for w in writes:
            w.w = op
            w.rs = []
        for r in reads:
            r.rs.append(op)
        self.ops[eng].append(op)
        return op

    def barrier(self):
        f = set()
        for e in ENGS:
            for op in reversed(self.ops[e]):
                if op.key is None:
                    f.add(op)
                    break
        f.update(self.last_dma.values())
        self.fence = f
        self.fence_seen = {}


def build(NP, NS, TP):
    nc = bass.Bass("TRN2", target_bir_lowering=False)
    S = Sched()
    NTP = TP // 128
    NTMAX = max(NTP, PAST // 128 + 1)
    TKMAX = NTMAX * 128

    def din(name, shape, dt=F32):
        return nc.dram_tensor(name, list(shape), dt, kind="ExternalInput").ap()

    def dout(name, shape, dt=F32):
        return nc.dram_tensor(name, list(shape), dt, kind="ExternalOutput").ap()

    xp = din("xp", [NP, TP, D])
    xs = din("xs", [NS, SSEQ, D])
    ck = din("ck", [NS, PAST, D])
    cv = din("cv", [NS, PAST, D])
    cl = din("cl", [NS, PAST, H])
    sc = din("sc", [NS, HIST, D])
    w_in = din("w_in", [D, IN_W])
    w_pa = din("w_pa", [D, D])
    w_pb = din("w_pb", [D, D])
    w_out = din("w_out", [D, D])
    g8_d = din("g8", [128, 8])
    bf_d = din("bf_rep", [128, H])
    gq_d = din("gq_col", [64, 1])
    gk_d = din("gk_rep", [128, 128])
    wdw_d = din("wdw", [128, 8, CONV_K])
    bdw_d = din("bdw8", [128, 8])
    lng_d = din("lng8", [128, 8])
    lnb_d = din("lnb8", [128, 8])

    yp = dout("yp", [NP, TP, D])
    ys = dout("ys", [NS, SSEQ, D])
    kpo = dout("kp", [NP, TP, D])
    vpo = dout("vp", [NP, TP, D])
    fpo = dout("fp", [NP, TP, H])
    cpo = dout("cp", [NP, HIST, D])
    kso = dout("ks", [NS, SSEQ, D])
    vso = dout("vs", [NS, SSEQ, D])
    fso = dout("fs", [NS, SSEQ, H])
    cso = dout("cs", [NS, HIST, D])

    blocks = {}
    off = 0

    def addblk(name, W, ncols=None):
        nonlocal off
        n = ncols if ncols is not None else 8 * W
        blocks[name] = (off, W, n)
        off += n

    for p in range(8):
        addblk(f"qkv{p}", 384)
    addblk("f", 16)
    for p in range(8):
        addblk(f"ga{p}", 128)
    for c in range(8):
        addblk(f"uab{c}", 256)
    for c in range(8):
        addblk(f"gb{c}", 128)
    for c in range(8):
        addblk(f"pm{c}", 512)
    addblk("wo0", 512)
    addblk("wo1", 512)
    SCR_COLS = off
    scr = nc.dram_tensor("wscr", [128, SCR_COLS], BF16, kind="Internal").ap()

    st = ExitStack()
    with st:
        def sb(name, shape, dt=F32):
            return st.enter_context(nc.sbuf_tensor(name, list(shape), dt))

        ident_f = sb("ident_f", [128, 128])
        ident_b = sb("ident_b", [128, 128], BF16)
        mask_f = sb("mask_f", [128, 128])
        mask_b = sb("mask_b", [128, 128], BF16)
        ones_f = sb("ones_f", [128, 128])
        ones_b = sb("ones_b", [128, 128], BF16)
        nmask_b = sb("nmask_b", [128, 128], BF16)
        g8 = sb("g8s", [128, 8])
        bf_rep = sb("bf_reps", [128, H])
        gk_rep = sb("gk_reps", [128, 128])
        kscale = sb("kscale", [128, 1])
        wdw = sb("wdws", [128, 8, CONV_K])
        bdw8 = sb("bdw8s", [128, 8])
        lng8 = sb("lng8s", [128, 8])
        lnb8 = sb("lnb8s", [128, 8])
        nlnb8 = sb("nlnb8", [128, 8])
        negh = sb("negh", [128, 8])
        negh512 = sb("negh512", [128, 512])
        hT = sb("hT", [128, 8, TP], BF16)
        og = sb("og", [128, 8, TP], BF16)
        lf = sb("lf", [128, NTMAX, H])
        negC = sb("negC", [128, NTMAX, H])
        Cp = sb("Cp", [128, NTMAX, H, 3], BF16)
        ARENA_W = 33280
        arena = sb("arena", [128, ARENA_W])
        banks = [st.enter_context(nc.psum_tensor(f"bank{i}", [128, 512], F32)) for i in range(8)]
        RB = [Res() for _ in range(8)]
        sems = {e: st.enter_context(nc.semaphore(f"sem_{e}")) for e in ("pe", "act", "dve", "pool")}
        dma_sems = {}

        def dsem(key):
            if key not in dma_sems:
                dma_sems[key] = st.enter_context(nc.semaphore(f"dq_{key}"))
            return dma_sems[key]

        R_const = Res()
        R_scr = Res()
        R_hT = Res()
        R_og = Res()
        R_lf = Res()
        R_negC = Res()
        R_Cp = Res()

        class Carver:
            def __init__(self):
                self.off = 0

            def reset(self):
                self.off = 0

            def get(self, shape, dt=F32):
                n = int(np.prod(shape[1:]))
                words = n if dt == F32 else (n + 1) // 2
                a = arena[:, self.off:self.off + words]
                self.off += words
                assert self.off <= ARENA_W, ("arena overflow", self.off)
                if dt != F32:
                    a = a.bitcast(dt)[:, 0:n]
                if len(shape) == 3:
                    a = a.rearrange("p (a b) -> p a b", a=shape[1], b=shape[2])
                elif len(shape) == 4:
                    a = a.rearrange("p (a b c) -> p a b c", a=shape[1], b=shape[2], c=shape[3])
                return a

        CV = Carver()

        def bbf(i):
            return banks[i][:].bitcast(BF16)

        def A_(out, in_, func, reads, writes, scale=1.0, bias=None):
            def fn(e):
                if bias is None:
                    return e.activation(out=out, in_=in_, func=func, scale=scale)
                return e.activation(out=out, in_=in_, func=func, scale=scale, bias=bias)
            return S.add("act", fn, reads, writes)

        def TT(eng, out, in0, in1, op, reads, writes):
            return S.add(eng, lambda e: e.tensor_tensor(out=out, in0=in0, in1=in1, op=op), reads, writes)

        def TS(eng, out, in0, s1, op0, reads, writes, s2=None, op1=None):
            if op1 is None:
                return S.add(eng, lambda e: e.tensor_scalar(out=out, in0=in0, scalar1=s1, scalar2=None, op0=op0), reads, writes)
            return S.add(eng, lambda e: e.tensor_scalar(out=out, in0=in0, scalar1=s1, scalar2=s2, op0=op0, op1=op1), reads, writes)

        def STT(out, in0, scalar, in1, op0, op1, reads, writes):
            return S.add("dve", lambda e: e.scalar_tensor_tensor(out=out, in0=in0, scalar=scalar, in1=in1, op0=op0, op1=op1), reads, writes)

        def CP(eng, out, in_, reads, writes):
            return S.add(eng, lambda e: e.tensor_copy(out=out, in_=in_), reads, writes)

        def MS(eng, ap, val, writes):
            return S.add(eng, lambda e: e.memset(ap, val), (), writes)

        def DMA(out, in_, reads, writes, key, slow=False, eng="sp"):
            def fn(q):
                if slow:
                    return q.dma_start(out=out, in_=in_, allow_slow_non_contiguous=True)
                return q.dma_start(out=out, in_=in_)
            return S.add(eng, fn, reads, writes, key=key)

        def DMA2(pairs, reads, writes, key, eng="sp"):
            def fn(q):
                return [q.dma_start(out=o, in_=i) for (o, i) in pairs]
            return S.add(eng, fn, reads, writes, key=key, ninc=len(pairs))

        PH = {"cur": "pro", "log": [], "n": 0}

        def phase(name):
            PH["log"].append((PH["cur"], PH["n"]))
            PH["cur"] = name

        def MM(groups, reads, writes):
            PH["n"] += len(groups)
            def fn(e):
                ins = None
                for (o, l, r, s0, s1) in groups:
                    ins = e.matmul(o, lhsT=l, rhs=r, start=s0, stop=s1)
                return ins
            return S.add("pe", fn, reads, writes)

        def TR(items, reads, writes):
            PH["n"] += len(items)

            def fn(e):
                ins = None
                for (o, i) in items:
                    ins = e.transpose(o, i, ident_b[:])
                return ins
            return S.add("pe", fn, reads, writes)

        def sigmoid_chain(buf, src, Rbuf, Rsrc, scale=-1.0, bias=None):
            A_(buf, src, AF.Exp, [Rsrc], [Rbuf], scale=scale, bias=bias)
            A_(buf, buf, AF.Ln, [Rbuf], [Rbuf], bias=1.0)
            A_(buf, buf, AF.Exp, [Rbuf], [Rbuf], scale=-1.0)

        MS("pool", ident_f[:], 1.0, [R_const])
        S.add("pool", lambda e: e.affine_select(out=ident_f[:], in_=ident_f[:], pattern=[[-1, 128]], compare_op=ALU.is_equal,
                                                fill=0.0, base=0, channel_multiplier=1), [R_const], [R_const])
        CP("pool", ident_b[:], ident_f[:], [R_const], [R_const])
        MS("pool", mask_f[:], 1.0, [R_const])
        S.add("pool", lambda e: e.affine_select(out=mask_f[:], in_=mask_f[:], pattern=[[1, 128]], compare_op=ALU.is_ge,
                                                fill=0.0, base=0, channel_multiplier=-1), [R_const], [R_const])
        CP("pool", mask_b[:], mask_f[:], [R_const], [R_const])
        MS("pool", ones_f[:], 1.0, [R_const])
        TS("pool", nmask_b[:], mask_f[:], 30000.0, ALU.mult, [R_const], [R_const], s2=-30000.0, op1=ALU.add)
        MS("pool", ones_b[:], 1.0, [R_const])
        MS("pool", negh[:], -0.5, [R_const])
        MS("pool", negh512[:], -0.5, [R_const])
        MS("pool", kscale[:], 1.0, [R_const])
        for (dst, src) in ((g8, g8_d), (bf_rep, bf_d), (gk_rep, gk_d), (bdw8, bdw_d), (lng8, lng_d), (lnb8, lnb_d)):
            DMA(dst[:], src[:, :], [], [R_const], "misc")
        DMA(wdw[:], wdw_d[:, :, :], [], [R_const], "misc")
        gq_t = sb("gq_t", [64, 1])
        DMA(gq_t[:], gq_d[:, :], [], [R_const], "misc")
        TS("pool", kscale[0:64, :], gq_t[:], float(HD ** -0.5), ALU.mult, [R_const], [R_const])
        TS("pool", nlnb8[:], lnb8[:], -1.0, ALU.mult, [R_const], [R_const])

        CV.reset()
        NSF = 4
        stg_f = [CV.get([128, 8, 512]) for _ in range(NSF)]
        stg_b = [CV.get([128, 4096], BF16) for _ in range(3)]
        R_sf = [Res() for _ in range(NSF)]
        R_sbb = [Res(), Res(), Res()]
        segcnt = [0]
        blkcnt = [0]

        def wsrc(m):
            return m.rearrange("(kc p) n -> p kc n", p=128)

        def prep_block(name, segs):
            o_, W, n = blocks[name]
            bs = blkcnt[0] % 3
            blkcnt[0] += 1
            o = 0
            for (src, c0, w, scaled) in segs:
                fs = segcnt[0] % NSF
                eng = "dve" if segcnt[0] % 3 != 2 else "pool"
                segcnt[0] += 1
                DMA(stg_f[fs][:, :, 0:w], wsrc(src)[:, :, c0:c0 + w], [], [R_sf[fs]], ("wB%d" % fs) if fs < 3 else "x0")
                outv = stg_b[bs][:, 0:8 * W].rearrange("p (a b) -> p a b", a=8, b=W)[:, :, o:o + w]
                if scaled:
                    TT(eng, outv, stg_f[fs][:, :, 0:w], g8[:].unsqueeze(2).broadcast_to([128, 8, w]), ALU.mult,
                       [R_sf[fs], R_const], [R_sbb[bs]])
                else:
                    CP(eng, outv, stg_f[fs][:, :, 0:w], [R_sf[fs]], [R_sbb[bs]])
                o += w
            DMA(scr[:, o_:o_ + n], stg_b[bs][:, 0:n], [R_sbb[bs]], [], ("wA%d" % bs) if bs < 2 else "x1", eng="act")

        for p in range(8):
            prep_block(f"qkv{p}", [(w_in, OFF_Q + 128 * p, 128, True), (w_in, OFF_K + 128 * p, 128, True), (w_in, OFF_V + 128 * p, 128, True)])
        prep_block("f", [(w_in, OFF_F, 16, True)])
        for p in range(8):
            prep_block(f"ga{p}", [(w_in, OFF_GA + 128 * p, 128, True)])
        for c in range(8):
            prep_block(f"uab{c}", [(w_in, OFF_UA + 128 * c, 128, True), (w_in, OFF_UB + 128 * c, 128, True)])
        for c in range(8):
            prep_block(f"gb{c}", [(w_in, OFF_GB + 128 * c, 128, True)])
        for c in range(8):
            prep_block(f"pm{c}", [(w_pa, 128 * c, 128, False), (w_pb, 128 * c, 128, False),
                                  (w_in, OFF_MA + 128 * c, 128, True), (w_in, OFF_MB + 128 * c, 128, True)])
        prep_block("wo0", [(w_out, 0, 512, False)])
        prep_block("wo1", [(w_out, 512, 512, False)])
        S.barrier()

        class Ring:
            def __init__(self, name, n, words):
                self.name = name
                self.n = n
                self.words = words
                self.bufs = None
                self.res = [Res() for _ in range(n)]
                self.cnt = 0
                self.plan = []
                self.issued = 0
                self.taken = 0
                self.slots = {}

            def carve(self):
                self.bufs = [CV.get([128, self.words], BF16) for _ in range(self.n)]

            def set_plan(self, names):
                self.plan = list(names)
                self.issued = 0
                self.taken = 0

            def _issue(self):
                i = self.issued
                o_, W, n = blocks[self.plan[i]]
                s = self.cnt % self.n
                self.cnt += 1
                DMA(self.bufs[s][:, 0:n], scr[:, o_:o_ + n], [], [self.res[s]], "misc" if self.name == "wF" else f"{self.name}{s}")
                self.slots[i] = (self.bufs[s][:, 0:n], self.res[s])
                self.issued += 1

            def next(self, k=1):
                i = self.taken
                while self.issued < min(len(self.plan), i + self.n):
                    self._issue()
                self.taken += k
                if k == 1:
                    return self.slots.pop(i)
                return [self.slots.pop(i + q) for q in range(k)]

        def w3(ap, a, b):
            return ap.rearrange("p (a b) -> p a b", a=a, b=b)

        import os as _os
        _kseq2 = _os.environ.get("KSEQ2", "")
        seqno = [-1]

        def run_sequence(kind, b):
            seqno[0] += 1
            isP = kind == "p"
            NTn = NTP if isP else 1
            NTp = 0 if isP else PAST // 128
            NT = NTn + NTp
            CW = 512 if isP else 128
            NCH = NTn * 128 // CW
            VR = 128 if isP else SSEQ
            xsrc = xp[b] if isP else xs[b]
            y_o = yp[b] if isP else ys[b]
            k_o = kpo[b] if isP else kso[b]
            v_o = vpo[b] if isP else vso[b]
            f_o = fpo[b] if isP else fso[b]
            c_o = cpo[b] if isP else cso[b]

            CV.reset()
            NXT = 4
            xt = [CV.get([128, D]) for _ in range(NXT)]
            R_xt = [Res() for _ in range(NXT)]
            sq1 = CV.get([128, D])
            R_sq1 = Res()
            ss1 = [CV.get([128, 2]) for _ in range(2)]
            R_ss1 = [Res(), Res()]
            xb = [CV.get([128, D], BF16) for _ in range(2)]
            R_xb = [Res(), Res()]
            wA = Ring("wA", 2, 8 * 384)
            wA.carve()
            wG = Ring("wG", 2, 8 * 128)
            wG.carve()
            wF = Ring("wF", 1, 8 * 16)
            wF.carve()
            wA.set_plan([f"qkv{p}" for p in range(8)])
            wG.set_plan([f"ga{p}" for p in range(8)])
            wF.set_plan(["f"])
            QTa = CV.get([128, 2, TKMAX], BF16)
            KTa = CV.get([128, 2, TKMAX], BF16)
            va = CV.get([128, NTMAX, 192], BF16)
            R_QT, R_KT, R_va = Res(), Res(), Res()
            va4 = va.rearrange("p t (a b) -> p t a b", a=3, b=64)
            NR = 4
            PPB = (0, 1, 7, 2)
            TRB = (4, 5, 6, 3)
            sq2 = [CV.get([128, 256]) for _ in range(NR)]
            ss2 = [CV.get([128, 4]) for _ in range(NR)]
            rs2 = [CV.get([128, 4]) for _ in range(NR)]
            kt = [CV.get([128, 128]) for _ in range(NR)]
            KVB = 4 if isP else 1
            NKV = 3
            koutB = [CV.get([128, KVB, 128]) for _ in range(NKV)]
            voutB = [CV.get([128, KVB, 128]) for _ in range(NKV)]
            R_koutB = [Res() for _ in range(NKV)]
            R_voutB = [Res() for _ in range(NKV)]
            kvb_cnt = [0]
            qa = [CV.get([128, 2, 68], BF16) for _ in range(NR)]
            ka = [CV.get([128, 2, 68], BF16) for _ in range(NR)]
            if not isP:
                kinA = CV.get([128, NTp, 128])
                vinA = CV.get([128, NTp, 128])
            R_kinA, R_vinA = Res(), Res()
            R_sq2 = [Res() for _ in range(NR)]
            R_ss2 = [Res() for _ in range(NR)]
            R_rs2 = [Res() for _ in range(NR)]
            R_kt = [Res() for _ in range(NR)]
            R_qa = [Res() for _ in range(NR)]
            R_qaC = [Res() for _ in range(NR)]
            R_ka = [Res() for _ in range(NR)]
            pfa = CV.get([128, 16 + NTMAX, H])
            pfb = CV.get([128, 16 + NTMAX, H])
            R_pfa, R_pfb = Res(), Res()
            NPR = 4
            Pb = [CV.get([128, 512], BF16) for _ in range(NPR)]
            R_P = [Res() for _ in range(NPR)]
            gt = [CV.get([128, 512]) for _ in range(2)]
            sg = [CV.get([128, 512]) for _ in range(2)]
            ldb = CV.get([128, 512])
            t2 = CV.get([128, 512])
            R_gt = [Res(), Res()]
            R_sg = [Res(), Res()]
            R_ld, R_t2 = Res(), Res()
            zz = CV.get([128, NTMAX * H])
            r1 = CV.get([128, NTMAX * H])
            R_zz, R_r1 = Res(), Res()

            MS("pool", va[:, :, 64:128], 1.0, [R_va])
            for s_ in range(NR):
                MS("pool", ka[s_][:, :, 64:68], 1.0, [R_ka[s_]])
            MS("pool", pfa[:, 0:16, :], 0.0, [R_pfa])
            MS("pool", pfb[:, 0:16, :], 0.0, [R_pfb])

            phase(kind + "A1")
            def a1_load(t):
                x_ = t % NXT
                if not isP:
                    MS("pool", xt[x_][:], 0.0, [R_xt[x_]])
                DMA(xt[x_][0:VR, :], xsrc[t * 128:t * 128 + VR, :], [], [R_xt[x_]], f"x{x_}")

            def a1_s1(t):
                s_ = t % 2
                x_ = t % NXT
                A_(sq1[:, :], xt[x_][:, :], AF.Square, [R_xt[x_]], [R_sq1])
                S.add("dve", (lambda o, i: (lambda e: e.tensor_reduce(out=o, in_=i, axis=AX, op=ALU.add)))(ss1[s_][:, 0:1], sq1[:, :]),
                      [R_sq1], [R_ss1[s_]])
                TS("pool", ss1[s_][:, 0:1], ss1[s_][:, 0:1], 1.0 / D, ALU.mult, [R_ss1[s_]], [R_ss1[s_]], s2=EPS, op1=ALU.add)
                TT("pool", ss1[s_][:, 1:2], ss1[s_][:, 0:1], negh[:, 0:1], ALU.pow, [R_ss1[s_], R_const], [R_ss1[s_]])

            def a1_s2(t):
                s_ = t % 2
                x_ = t % NXT
                TS("dve", xb[s_][:, :], xt[x_][:, :], ss1[s_][:, 1:2], ALU.mult, [R_xt[x_], R_ss1[s_]], [R_xb[s_]])
                if t + NXT < NTn:
                    a1_load(t + NXT)
                bk = 4 + s_
                trv = w3(bbf(bk), 8, 128)
                TR([(trv[:, kc, :], xb[s_][:, kc * 128:(kc + 1) * 128]) for kc in range(8)], [R_xb[s_], R_const], [RB[bk]])

            def a1_s3(t):
                s_ = t % 2
                bk = 4 + s_
                trv = w3(bbf(bk), 8, 128)
                CP("dve", hT[:, :, t * 128:(t + 1) * 128], trv, [RB[bk]], [R_hT])
                MM([(banks[2][:, t * 16:(t + 1) * 16], hT[:, kc, t * 128:(t + 1) * 128], wfv[:, kc, :], kc == 0, kc == 7) for kc in range(8)],
                   [R_hT, R_wf], [RB[2]])

            wf_ap, R_wf = wF.next()
            wfv = w3(wf_ap, 8, 16)
            for t_ in range(min(NXT, NTn)):
                a1_load(t_)
            a1_s1(0)
            for t in range(NTn):
                a1_s2(t)
                if t + 1 < NTn:
                    a1_s1(t + 1)
                a1_s3(t)

            if seqno[0] >= 1 and _kseq2 == "A1":
                S.barrier()
                return
            phase(kind + "A2")
            zv = w3(zz[:, 0:NTn * H], NTn, H)
            TT("dve", zv, w3(banks[2][:, 0:NTn * H], NTn, H), bf_rep[:].unsqueeze(1).broadcast_to([128, NTn, H]), ALU.add,
               [RB[2], R_const], [R_zz])
            A_(zz[:, 0:NTn * H], zz[:, 0:NTn * H], AF.Exp, [R_zz], [R_zz], scale=-1.0)
            A_(zz[:, 0:NTn * H], zz[:, 0:NTn * H], AF.Ln, [R_zz], [R_zz], bias=1.0)
            if not isP:
                for t_ in range(NTp):
                    DMA(lf[:, t_, :], cl[b, t_ * 128:(t_ + 1) * 128, :], [], [R_lf], "misc")
            TS("dve", lf[:, NTp:NT, :], zv, -1.0, ALU.mult, [R_zz], [R_lf])
            if isP:
                for t_ in range(NT):
                    DMA(f_o[t_ * 128:(t_ + 1) * 128, :], lf[:, t_, :], [R_lf], [], "ast", eng="act")
            else:
                DMA(f_o[:, :], lf[0:VR, NTp, :], [R_lf], [], "ast", eng="act")
            PAD = 16
            CP("dve", pfa[:, PAD:PAD + 1, :], lf[:, 0:1, :], [R_lf], [R_pfa])
            if NT > 1:
                TT("dve", pfa[:, PAD + 1:PAD + NT, :], lf[:, 1:NT, :], lf[:, 0:NT - 1, :], ALU.add, [R_lf], [R_pfa])
            cur, Rcur, nxt, Rnxt = pfa, R_pfa, pfb, R_pfb
            sh = 2
            while sh < NT:
                TT("dve", nxt[:, PAD:PAD + NT, :], cur[:, PAD:PAD + NT, :], cur[:, PAD - sh:PAD + NT - sh, :], ALU.add, [Rcur], [Rnxt])
                cur, Rcur, nxt, Rnxt = nxt, Rnxt, cur, Rcur
                sh *= 2
            MM([(banks[3][:, 0:NT * H], ones_f[:], cur[:, PAD - 1:PAD - 1 + NT, :], True, False),
                (banks[3][:, 0:NT * H], mask_f[:], lf[:, 0:NT, :], False, True)], [R_lf, Rcur, R_const], [RB[3]])
            cps = w3(banks[3][:, 0:NT * H], NT, H)
            TS("dve", negC[:, 0:NT, :], cps, -1.0, ALU.mult, [RB[3]], [R_negC])
            r1v = w3(r1[:, 0:NT * H], NT, H)
            CP("dve", Cp[:, 0:NT, :, 0], cps, [RB[3]], [R_Cp])
            TT("dve", r1v, cps, Cp[:, 0:NT, :, 0], ALU.subtract, [RB[3], R_Cp], [R_r1])
            CP("dve", Cp[:, 0:NT, :, 1], r1v, [R_r1], [R_Cp])
            TT("dve", r1v, r1v, Cp[:, 0:NT, :, 1], ALU.subtract, [R_r1, R_Cp], [R_r1])
            CP("dve", Cp[:, 0:NT, :, 2], r1v, [R_r1], [R_Cp])

            if seqno[0] >= 1 and _kseq2 == "A2":
                S.barrier()
                return
            kvcnt = [0]
            for p in range(8):
                wq_ap, R_wq = wA.next()
                wqv = w3(wq_ap, 8, 384)
                wg_ap, R_wg = wG.next()
                wgv = w3(wg_ap, 8, 128)
                if NTp:
                    DMA2([(kinA[:, :, :], ck[b, :, p * 128:(p + 1) * 128].rearrange("(n q) c -> q n c", q=128)),
                          (vinA[:, :, :], cv[b, :, p * 128:(p + 1) * 128].rearrange("(n q) c -> q n c", q=128))], [], [R_kinA, R_vinA], "kvi0")
                for t in range(NTp):
                    s_ = kvcnt[0] % NR
                    kvcnt[0] += 1
                    CP("dve", ka[s_][:, :, 0:64], w3(kinA[:, t, :], 2, 64), [R_kinA], [R_ka[s_]])
                    CP("pool", va[:, t, 0:64], vinA[:, t, 0:64], [R_vinA], [R_va])
                    CP("pool", va[:, t, 128:192], vinA[:, t, 64:128], [R_vinA], [R_va])
                    bk = TRB[s_]
                    trv = w3(bbf(bk)[:, 0:512], 4, 128)
                    TR([(trv[0:67, 2 + hh, :], ka[s_][:, hh, 0:67]) for hh in range(2)], [R_ka[s_], R_const], [RB[bk]])
                    A_(KTa[0:67, :, t * 128:(t + 1) * 128], trv[0:67, 2:4, :], AF.Identity, [RB[bk], R_const], [R_KT], scale=kscale[0:67, 0:1])

                def proj(t):
                    bk = PPB[t % NR]
                    MM([(banks[bk][:, 0:384], hT[:, kc, t * 128:(t + 1) * 128], wqv[:, kc, :], kc == 0, kc == 7) for kc in range(8)],
                       [R_hT, R_wq], [RB[bk]])

                def kvslot(t):
                    return (kvbase[0] + t // KVB) % NKV

                def kout_t(t):
                    return koutB[kvslot(t)][:, t % KVB, :]

                def vout_t(t):
                    return voutB[kvslot(t)][:, t % KVB, :]

                def epi1(t):
                    bk = PPB[t % NR]
                    s_ = t % NR
                    pp = banks[bk]
                    A_(sq2[s_][:, :], pp[:, 0:256], AF.Square, [RB[bk]], [R_sq2[s_]])
                    A_(vout_t(t), pp[:, 256:384], AF.Copy, [RB[bk]], [R_voutB[kvslot(t)]])
                    S.add("dve", (lambda o, i: (lambda e: e.tensor_reduce(out=o, in_=i, axis=AX, op=ALU.add)))(ss2[s_][:, :], w3(sq2[s_][:, :], 4, 64)),
                          [R_sq2[s_]], [R_ss2[s_]])
                    A_(rs2[s_][:, :], ss2[s_][:, :], AF.Ln, [R_ss2[s_]], [R_rs2[s_]], scale=1.0 / HD, bias=EPS)
                    A_(rs2[s_][:, :], rs2[s_][:, :], AF.Exp, [R_rs2[s_]], [R_rs2[s_]], scale=-0.5)

                def epi2(t):
                    bk = PPB[t % NR]
                    s_ = t % NR
                    T = NTp + t
                    pp = banks[bk]
                    TT("dve", qa[s_][:, :, 0:64], w3(pp[:, 0:128], 2, 64), rs2[s_][:, 0:2].unsqueeze(2).broadcast_to([128, 2, 64]), ALU.mult,
                       [RB[bk], R_rs2[s_]], [R_qa[s_]])
                    CP("pool", qa[s_][:, :, 64:67], Cp[:, T, 2 * p:2 * p + 2, :], [R_Cp], [R_qaC[s_]])
                    TT("dve", w3(kt[s_][:, :], 2, 64), w3(pp[:, 128:256], 2, 64), rs2[s_][:, 2:4].unsqueeze(2).broadcast_to([128, 2, 64]), ALU.mult,
                       [RB[bk], R_rs2[s_]], [R_kt[s_]])
                    ks_ = kvslot(t)
                    TT("dve", kout_t(t), kt[s_][:, :], gk_rep[:, :], ALU.mult, [R_kt[s_], R_const], [R_koutB[ks_]])
                    CP("dve", ka[s_][:, :, 0:64], w3(kout_t(t), 2, 64), [R_koutB[ks_]], [R_ka[s_]])
                    CP("pool", va4[:, T, 0:3:2, :], w3(vout_t(t), 2, 64), [R_voutB[ks_]], [R_va])
                    if t % KVB == KVB - 1:
                        t0_ = t - (KVB - 1)
                        if isP:
                            kdst = k_o[t0_ * 128:(t + 1) * 128, p * 128:(p + 1) * 128].rearrange("(n q) c -> q n c", q=128)
                            vdst = v_o[t0_ * 128:(t + 1) * 128, p * 128:(p + 1) * 128].rearrange("(n q) c -> q n c", q=128)
                            DMA2([(kdst, koutB[ks_][:, :, :]), (vdst, voutB[ks_][:, :, :])], [R_koutB[ks_], R_voutB[ks_]], [], f"kvo{ks_}", eng="act")
                        else:
                            DMA2([(k_o[0:VR, p * 128:(p + 1) * 128], koutB[ks_][0:VR, 0, :]),
                                  (v_o[0:VR, p * 128:(p + 1) * 128], voutB[ks_][0:VR, 0, :])], [R_koutB[ks_], R_voutB[ks_]], [], f"kvo{ks_}", eng="act")

                def trans(t):
                    s_ = t % NR
                    T = NTp + t
                    bk = TRB[s_]
                    trv = w3(bbf(bk)[:, 0:512], 4, 128)
                    items = [(trv[0:67, hh, :], qa[s_][:, hh, 0:67]) for hh in range(2)]
                    items += [(trv[0:67, 2 + hh, :], ka[s_][:, hh, 0:67]) for hh in range(2)]
                    TR(items, [R_qa[s_], R_qaC[s_], R_ka[s_], R_const], [RB[bk]])
                    A_(QTa[0:67, :, T * 128:(T + 1) * 128], trv[0:67, 0:2, :], AF.Copy, [RB[bk]], [R_QT])
                    A_(KTa[0:67, :, T * 128:(T + 1) * 128], trv[0:67, 2:4, :], AF.Identity, [RB[bk], R_const], [R_KT], scale=kscale[0:67, 0:1])

                phase(kind + "A3")
                kvbase = [kvb_cnt[0]]
                kvb_cnt[0] += (NTn + KVB - 1) // KVB
                for t in range(min(3, NTn)):
                    proj(t)
                for t in range(min(2, NTn)):
                    epi1(t)
                epi2(0)
                for t in range(NTn):
                    if t + 3 < NTn:
                        proj(t + 3)
                    if t + 2 < NTn:
                        epi1(t + 2)
                    if t + 1 < NTn:
                        epi2(t + 1)
                    trans(t)

                phase(kind + "A4")
                SB = (7, 0, 1)

                fillers = []

                def ga_ops(G, now=False):
                    cols = slice(G * CW, (G + 1) * CW)
                    g_ = G % 2
                    ops_ = []
                    for kc in range(8):
                        ops_.append((lambda kc=kc: MM([(banks[6][:, 0:CW], wgv[:, kc, :], hT[:, kc, cols], kc == 0, kc == 7)], [R_hT, R_wg], [RB[6]])))
                    ops_.append(lambda: A_(gt[g_][:, 0:CW], banks[6][:, 0:CW], AF.Exp, [RB[6]], [R_gt[g_]], scale=-1.0))
                    ops_.append(lambda: A_(gt[g_][:, 0:CW], gt[g_][:, 0:CW], AF.Ln, [R_gt[g_]], [R_gt[g_]], bias=1.0))
                    ops_.append(lambda: A_(gt[g_][:, 0:CW], gt[g_][:, 0:CW], AF.Exp, [R_gt[g_]], [R_gt[g_]], scale=-1.0))
                    ops_.append(lambda: TT("dve", sg[g_][:, 0:CW], banks[6][:, 0:CW], gt[g_][:, 0:CW], ALU.mult, [RB[6], R_gt[g_]], [R_sg[g_]]))
                    if now:
                        for f_ in ops_:
                            f_()
                    else:
                        fillers.extend(ops_)

                def norm_ops(G, now=False):
                    cols = slice(G * CW, (G + 1) * CW)
                    g_ = G % 2
                    oa, ob_ = (2, 3) if G % 2 == 0 else (4, 5)
                    ops_ = [
                        lambda: S.add("dve", lambda e: e.reciprocal(out=ldb[0:64, 0:CW], in_=banks[oa][64:128, 0:CW]), [RB[oa]], [R_ld]),
                        lambda: S.add("dve", lambda e: e.reciprocal(out=ldb[64:128, 0:CW], in_=banks[ob_][0:64, 0:CW]), [RB[ob_]], [R_ld]),
                        lambda: TT("dve", t2[:, 0:CW], sg[g_][:, 0:CW], ldb[:, 0:CW], ALU.mult, [R_sg[g_], R_ld], [R_t2]),
                        lambda: TT("dve", og[0:64, p, cols], banks[oa][0:64, 0:CW], t2[0:64, 0:CW], ALU.mult, [RB[oa], R_t2], [R_og]),
                        lambda: TT("dve", og[64:128, p, cols], banks[ob_][64:128, 0:CW], t2[64:128, 0:CW], ALU.mult, [RB[ob_], R_t2], [R_og]),
                    ]
                    if now:
                        for f_ in ops_:
                            f_()
                    else:
                        fillers.extend(ops_)

                blks = []
                for G in range(NCH):
                    first_diag = NTp + G * CW // 128
                    nkb = first_diag + CW // 128
                    for hh in range(2):
                        for j in range(nkb):
                            blks.append(dict(G=G, hh=hh, j=j, c0=max(0, j - first_diag) * 128, diag=j >= first_diag,
                                             ob=((2, 3) if G % 2 == 0 else (4, 5))[hh], hoff=0 if hh == 0 else 64,
                                             head=2 * p + hh, first=j == 0, last=j == nkb - 1, qbase=NTp * 128 + G * CW))

                def qk(i):
                    bl = blks[i]
                    sbk = SB[i % 3]
                    c0 = bl["c0"]
                    j = bl["j"]
                    hh = bl["hh"]
                    kT = KTa[0:67, hh, j * 128:(j + 1) * 128]
                    qb = bl["qbase"]
                    if bl["diag"]:
                        grp = [(banks[sbk][:, c0:c0 + 128], ident_b[:, :], nmask_b[:, :], True, False),
                               (banks[sbk][:, c0:c0 + 128], kT, QTa[0:67, hh, qb + c0:qb + c0 + 128], False, True)]
                        if c0 + 128 < CW:
                            grp.append((banks[sbk][:, c0 + 128:CW], kT, QTa[0:67, hh, qb + c0 + 128:qb + CW], True, True))
                        MM(grp, [R_KT, R_QT, R_const], [RB[sbk]])
                    else:
                        MM([(banks[sbk][:, c0:CW], kT, QTa[0:67, hh, qb + c0:qb + CW], True, True)], [R_KT, R_QT], [RB[sbk]])

                ga_ops(0)
                nb = len(blks)
                qk(0)
                if nb > 1:
                    qk(1)
                for i in range(nb):
                    bl = blks[i]
                    if i + 2 < nb:
                        qk(i + 2)
                    c0 = bl["c0"]
                    j = bl["j"]
                    sbk = SB[i % 3]
                    ps_ = i % NPR
                    A_(Pb[ps_][:, c0:CW], banks[sbk][:, c0:CW], AF.Exp, [RB[sbk], R_negC], [R_P[ps_]], bias=negC[:, j, bl["head"]:bl["head"] + 1])
                    MM([(banks[bl["ob"]][:, c0:CW], va[:, j, bl["hoff"]:bl["hoff"] + 128], Pb[ps_][:, c0:CW], bl["first"], bl["last"])],
                       [R_va, R_P[ps_]], [RB[bl["ob"]]])
                    if bl["hh"] == 0 and j == 1 and bl["G"] + 1 < NCH:
                        ga_ops(bl["G"] + 1)
                    if bl["hh"] == 1 and bl["last"]:
                        norm_ops(bl["G"])
                    if fillers:
                        fillers.pop(0)()
                    if len(fillers) > 10:
                        fillers.pop(0)()
                while fillers:
                    fillers.pop(0)()

            S.barrier()
            if seqno[0] >= 1 and _kseq2 == "A":
                return
            CV.reset()
            wB = Ring("wB", 3, 4096)
            wB.carve()
            plan = []
            plan += [f"uab{c}" for c in range(8)]
            for _g in range(NCH):
                for c in range(8):
                    plan.append(f"gb{c}")
                    if _g + 1 < NCH:
                        plan.append(f"uab{c}")
                plan += [f"pm{c}" for c in range(8)] + ["wo0", "wo1"]
            wB.set_plan(plan)
            dgb = [CV.get([128, CONV_K, 128], BF16) for _ in range(2)]
            R_dgb = [Res(), Res()]
            u = CV.get([128, 8, HIST + 512], BF16)
            ufp = CV.get([128, 8, 32])
            conv = CV.get([128, 8, 512])
            cbf = [CV.get([128, 512], BF16) for _ in range(2)]
            csq = [CV.get([128, 512], BF16) for _ in range(2)]
            mean = CV.get([128, 512])
            msq = CV.get([128, 512])
            rstd = CV.get([128, 512])
            Bm = CV.get([128, 512])
            tb = [CV.get([128, 512]) for _ in range(2)]
            tb2 = [CV.get([128, 512]) for _ in range(2)]
            tq = CV.get([128, 512])
            v0 = CV.get([128, 512])
            sv = CV.get([128, 512])
            t3 = CV.get([128, 512])
            zb = CV.get([128, 8, 512], BF16)
            mT = CV.get([128, 8, 512], BF16)
            NXR = 4
            xr = [CV.get([128, D]) for _ in range(NXR)]
            tailT = CV.get([128, D])
            R_tail = Res()
            R_u, R_ufp, R_conv = Res(), Res(), Res()
            R_cbf = [Res(), Res()]
            R_csq = [Res(), Res()]
            R_mean, R_msq, R_rstd, R_Bm = Res(), Res(), Res(), Res()
            R_tb = [Res(), Res()]
            R_tb2 = [Res(), Res()]
            R_tq, R_v0, R_sv, R_t3, R_zb, R_m = Res(), Res(), Res(), Res(), Res(), Res()
            R_xr = [Res() for _ in range(NXR)]
            prebuilt = [False]
            tcnt = [0]
            xcnt = [0]

            def hist_for(G):
                if isP:
                    if G == 0:
                        MS("pool", u[:, :, 0:HIST], 0.0, [R_u])
                    else:
                        CP("pool", u[:, :, 0:HIST], u[:, :, CW:CW + HIST], [R_u], [R_u])
                else:
                    DMA(tailT[0:HIST, :], sc[b][:, :], [], [R_tail], "misc")
                    hv = banks[6][:, 0:256].rearrange("p (a b) -> p a b", a=8, b=32)
                    S.add("pe", (lambda hv_: (lambda e: [e.transpose(hv_[:, c_, 0:HIST], tailT[0:HIST, c_ * 128:(c_ + 1) * 128], ident_f[0:HIST, 0:HIST]) for c_ in range(8)][-1]))(hv),
                          [R_tail, R_const], [RB[6]])
                    TS("dve", u[:, :, 0:HIST], hv[:, :, 0:HIST], 2.0, ALU.mult, [RB[6]], [R_u])

            def b1_chunk(G, c):
                cols_ = slice(G * CW, (G + 1) * CW)
                last_ = G == NCH - 1
                wb_ap, R_wb = wB.next()
                wv = w3(wb_ap, 8, 256)
                ba = (2 * c) % 4
                bb_ = ba + 1
                MM([(banks[ba][:, 0:CW], wv[:, kc, 0:128], hT[:, kc, cols_], kc == 0, kc == 7) for kc in range(8)], [R_hT, R_wb], [RB[ba]])
                MM([(banks[bb_][:, 0:CW], wv[:, kc, 128:256], hT[:, kc, cols_], kc == 0, kc == 7) for kc in range(8)], [R_hT, R_wb], [RB[bb_]])
                ts_ = tcnt[0] % 2
                tcnt[0] += 1
                A_(tb[ts_][:, 0:CW], banks[bb_][:, 0:CW], AF.Tanh, [RB[bb_]], [R_tb[ts_]], scale=0.5)
                STT(u[:, c, HIST:HIST + CW], tb[ts_][:, 0:CW], 1.0, banks[ba][:, 0:CW], ALU.add, ALU.mult, [RB[ba], R_tb[ts_]], [R_u])
                if last_:
                    o32 = CW - 32 if isP else 0
                    STT(ufp[:, c, :], tb[ts_][:, o32:o32 + 32], 1.0, banks[ba][:, o32:o32 + 32], ALU.add, ALU.mult, [RB[ba], R_tb[ts_]], [R_ufp])

            def b1_tail(G):
                if G == NCH - 1:
                    S.add("pe", lambda e: [e.transpose(banks[2 + c_ // 4][0:32, (c_ % 4) * 128:(c_ % 4 + 1) * 128], ufp[:, c_, :], ident_f[:, :]) for c_ in range(8)][-1],
                          [R_ufp, R_const], [RB[2], RB[3]])
                    A_(tailT[0:32, 0:512], banks[2][0:32, :], AF.Identity, [RB[2]], [R_tail], scale=0.5)
                    A_(tailT[0:32, 512:1024], banks[3][0:32, :], AF.Identity, [RB[3]], [R_tail], scale=0.5)
                    DMA(c_o[:, :], tailT[2:32, :], [R_tail], [], "ast", eng="act")

            for G in range(NCH):
                cols = slice(G * CW, (G + 1) * CW)
                last = G == NCH - 1
                if G == 0:
                    hist_for(0)
                    phase(kind + "B1")
                    for c in range(8):
                        b1_chunk(0, c)
                    b1_tail(0)
                phase(kind + "B2")
                def stats_mm(c):
                    s_ = c % 2
                    MM([(banks[6][:, 0:CW], ones_b[:, :], cbf[s_][:, 0:CW], c == 0, c == 7)], [R_cbf[s_], R_const], [RB[6]])
                    MM([(banks[7][:, 0:CW], ones_b[:, :], csq[s_][:, 0:CW], c == 0, c == 7)], [R_csq[s_], R_const], [RB[7]])

                def mkdiag(c):
                    d_ = c % 2
                    TT("dve", dgb[d_][:, :, :], ident_b[:].unsqueeze(1).broadcast_to([128, CONV_K, 128]),
                       wdw[:, c, :].unsqueeze(2).broadcast_to([128, CONV_K, 128]), ALU.mult, [R_const], [R_dgb[d_]])

                if not prebuilt[0]:
                    mkdiag(0)
                for c in range(8):
                    d_ = c % 2
                    if c + 1 < 8 and not (c == 0 and prebuilt[0]):
                        mkdiag(c + 1)
                    wdv, R_wd = dgb[d_], R_dgb[d_]
                    bk = 4 + c % 2
                    MM([(banks[bk][:, 0:CW], wdv[:, j, :], u[:, c, j:j + CW], j == 0, j == CONV_K - 1) for j in range(CONV_K)], [R_u, R_wd], [RB[bk]])
                    if c > 0:
                        stats_mm(c - 1)
                    s_ = c % 2
                    A_(conv[:, c, 0:CW], banks[bk][:, 0:CW], AF.Identity, [RB[bk], R_const], [R_conv], scale=0.5, bias=bdw8[:, c:c + 1])
                    A_(csq[s_][:, 0:CW], banks[bk][:, 0:CW], AF.Square, [RB[bk], R_const], [R_csq[s_]], scale=0.5, bias=bdw8[:, c:c + 1])
                    CP("dve", cbf[s_][:, 0:CW], conv[:, c, 0:CW], [R_conv], [R_cbf[s_]])
                stats_mm(7)
                TS("dve", mean[:, 0:CW], banks[6][:, 0:CW], 1.0 / D, ALU.mult, [RB[6]], [R_mean])
                TT("dve", msq[:, 0:CW], mean[:, 0:CW], mean[:, 0:CW], ALU.mult, [R_mean], [R_msq])
                STT(msq[:, 0:CW], banks[7][:, 0:CW], 1.0 / D, msq[:, 0:CW], ALU.mult, ALU.subtract, [RB[7], R_msq], [R_msq])
                A_(rstd[:, 0:CW], msq[:, 0:CW], AF.Ln, [R_msq], [R_rstd], bias=EPS)
                A_(rstd[:, 0:CW], rstd[:, 0:CW], AF.Exp, [R_rstd], [R_rstd], scale=-0.5)
                STT(Bm[:, 0:CW], mean[:, 0:CW], -1.0, rstd[:, 0:CW], ALU.mult, ALU.mult, [R_mean, R_rstd], [R_Bm])
                if G + 1 < NCH:
                    hist_for(G + 1)
                phase(kind + "B3")
                for c in range(8):
                    wb_ap, R_wb = wB.next()
                    wv = w3(wb_ap, 8, 128)
                    bk = 6 + c % 2
                    MM([(banks[bk][:, 0:CW], wv[:, kc, :], hT[:, kc, cols], kc == 0, kc == 7) for kc in range(8)], [R_hT, R_wb], [RB[bk]])
                    STT(tq[:, 0:CW], conv[:, c, 0:CW], lng8[:, c:c + 1], rstd[:, 0:CW], ALU.mult, ALU.mult, [R_conv, R_rstd, R_const], [R_tq])
                    STT(v0[:, 0:CW], Bm[:, 0:CW], lng8[:, c:c + 1], tq[:, 0:CW], ALU.mult, ALU.add, [R_Bm, R_tq, R_const], [R_v0])
                    ts_ = tcnt[0] % 2
                    tcnt[0] += 1
                    TS("dve", v0[:, 0:CW], v0[:, 0:CW], lnb8[:, c:c + 1], ALU.add, [R_v0, R_const], [R_v0])
                    A_(tb[ts_][:, 0:CW], v0[:, 0:CW], AF.Tanh, [R_v0], [R_tb[ts_]], scale=0.5)
                    STT(sv[:, 0:CW], tb[ts_][:, 0:CW], 1.0, v0[:, 0:CW], ALU.add, ALU.mult, [R_v0, R_tb[ts_]], [R_sv])
                    A_(tb2[ts_][:, 0:CW], banks[bk][:, 0:CW], AF.Tanh, [RB[bk]], [R_tb2[ts_]], scale=0.5)
                    STT(t3[:, 0:CW], tb2[ts_][:, 0:CW], 1.0, banks[bk][:, 0:CW], ALU.add, ALU.mult, [RB[bk], R_tb2[ts_]], [R_t3])
                    STT(zb[:, c, 0:CW], t3[:, 0:CW], 0.25, sv[:, 0:CW], ALU.mult, ALU.mult, [R_t3, R_sv], [R_zb])
                    if G + 1 < NCH:
                        b1_chunk(G + 1, c)
                if G + 1 < NCH:
                    b1_tail(G + 1)
                phase(kind + "B4")
                for c in range(8):
                    wb_ap, R_wb = wB.next()
                    wv = w3(wb_ap, 8, 512)
                    base = 0 if c % 2 == 0 else 4
                    MM([(banks[base][:, 0:CW], wv[:, kc, 0:128], og[:, kc, cols], kc == 0, kc == 7) for kc in range(8)], [R_og, R_wb], [RB[base]])
                    MM([(banks[base + 1][:, 0:CW], wv[:, kc, 128:256], zb[:, kc, 0:CW], kc == 0, kc == 7) for kc in range(8)], [R_zb, R_wb], [RB[base + 1]])
                    MM([(banks[base + 2][:, 0:CW], wv[:, kc, 256:384], hT[:, kc, cols], kc == 0, kc == 7) for kc in range(8)], [R_hT, R_wb], [RB[base + 2]])
                    MM([(banks[base + 3][:, 0:CW], wv[:, kc, 384:512], hT[:, kc, cols], kc == 0, kc == 7) for kc in range(8)], [R_hT, R_wb], [RB[base + 3]])
                    ts_ = tcnt[0] % 2
                    tcnt[0] += 1
                    A_(tb[ts_][:, 0:CW], banks[base + 2][:, 0:CW], AF.Tanh, [RB[base + 2]], [R_tb[ts_]], scale=0.5)
                    A_(tb2[ts_][:, 0:CW], banks[base + 3][:, 0:CW], AF.Tanh, [RB[base + 3]], [R_tb2[ts_]], scale=0.5)
                    STT(tq[:, 0:CW], tb[ts_][:, 0:CW], 1.0, banks[base][:, 0:CW], ALU.add, ALU.mult, [RB[base], R_tb[ts_]], [R_tq])
                    STT(t3[:, 0:CW], tb2[ts_][:, 0:CW], 1.0, banks[base + 1][:, 0:CW], ALU.add, ALU.mult, [RB[base + 1], R_tb2[ts_]], [R_t3])
                    TT("dve", mT[:, c, 0:CW], tq[:, 0:CW], t3[:, 0:CW], ALU.add, [R_tq, R_t3], [R_m])
                prebuilt[0] = G + 1 < NCH
                phase(kind + "B5")
                wo = [(w3(a_, 8, 512), r_) for (a_, r_) in wB.next(2)]
                for tt in range(CW // 128):
                    xs_ = xcnt[0] % NXR
                    xcnt[0] += 1
                    trow = G * CW + tt * 128
                    DMA(xr[xs_][0:VR, :], xsrc[trow:trow + VR, :], [], [R_xr[xs_]], f"x{xs_}")
                    for hf in range(2):
                        bk = (2 * tt + hf) % 8
                        MM([(banks[bk][:, 0:512], mT[:, kc, tt * 128:(tt + 1) * 128], wo[hf][0][:, kc, :], kc == 0, kc == 7) for kc in range(8)],
                           [R_m, wo[hf][1]], [RB[bk]])
                        STT(xr[xs_][0:VR, hf * 512:(hf + 1) * 512], banks[bk][0:VR, 0:512], 0.5, xr[xs_][0:VR, hf * 512:(hf + 1) * 512], ALU.mult, ALU.add,
                            [RB[bk], R_xr[xs_]], [R_xr[xs_]])
                    DMA(y_o[trow:trow + VR, :], xr[xs_][0:VR, :], [R_xr[xs_]], [], f"yo{xs_ % 2}", eng="act")
                    if prebuilt[0] and tt == 0:
                        mkdiag(0)
                    if prebuilt[0] and tt == 2:
                        mkdiag(1)
            S.barrier()

        import os as _os
        _kstop = _os.environ.get("KSTOP", "")
        for b in range(NS):
            if _kstop not in ("pro", "ponly"):
                run_sequence("s", b)
        for b in range(NP):
            if _kstop not in ("pro", "sonly"):
                run_sequence("p", b)

        cnt = {e: 0 for e in ("pe", "act", "dve", "pool")}
        dcnt = {}
        for e in ENGS:
            for op in S.ops[e]:
                if op.key is not None:
                    dcnt[op.key] = dcnt.get(op.key, 0) + 16 * op.ninc
                    op.sigval = dcnt[op.key]
                    dsem(op.key)
                elif op.sig:
                    cnt[e] += 1
                    op.sigval = cnt[e]

        def emit(eng, h):
            waited = {}
            for op in S.ops[eng]:
                need = {}
                for d in op.deps:
                    k = ("d", d.key) if d.key is not None else ("e", d.eng)
                    if d.sigval > need.get(k, 0):
                        need[k] = d.sigval
                for k, v in need.items():
                    if waited.get(k, 0) < v:
                        sem = dma_sems[k[1]] if k[0] == "d" else sems[k[1]]
                        h.wait_ge(sem, v)
                        waited[k] = v
                ins = op.fn(h)
                if op.key is not None:
                    for i_ in (ins if isinstance(ins, (list, tuple)) else [ins]):
                        i_.then_inc(dma_sems[op.key], 16)
                elif op.sig:
                    ins.then_inc(sems[eng], 1)
            if eng == "sp":
                for k, v in dcnt.items():
                    h.wait_ge(dma_sems[k], v)

        global LAST_STATS
        LAST_STATS = {e: len(S.ops[e]) for e in ENGS}
        LAST_STATS["sems"] = len(dma_sems) + 4
        phase("end")
        LAST_STATS["phases"] = list(PH["log"])
        with nc.Block() as block:
            @block.tensor
            def _(t):
                emit("pe", t)

            @block.scalar
            def _(a):
                emit("act", a)

            @block.vector
            def _(v):
                emit("dve", v)

            @block.gpsimd
            def _(g):
                emit("pool", g)

            @block.sync
            def _(s):
                emit("sp", s)
    return nc


_NC_CACHE = {}
LAST_STATS = {}


def _get_nc(NP, NS, TP):
    k = (NP, NS, TP)
    if k not in _NC_CACHE:
        _NC_CACHE[k] = build(NP, NS, TP)
    return _NC_CACHE[k]


def _common_maps(norm_g, w_in, b_f, q_g, k_g, w_dw, b_dw, ln_g, ln_b, w_pa, w_pb, w_out):
    f = np.float32
    c = lambda a: np.ascontiguousarray(a, dtype=f)
    return {
        "w_in": c(w_in[0]), "w_pa": c(w_pa[0]), "w_pb": c(w_pb[0]), "w_out": c(w_out[0]),
        "g8": c(norm_g[0].reshape(8, 128).T),
        "bf_rep": c(np.tile(b_f[0][None, :], (128, 1))),
        "gq_col": c(q_g[0].reshape(64, 1)),
        "gk_rep": c(np.tile(k_g[0][None, :], (128, 2))),
        "wdw": c(w_dw[0].reshape(CONV_K, 8, 128).transpose(2, 1, 0)),
        "bdw8": c(b_dw[0].reshape(8, 128).T),
        "lng8": c(ln_g[0].reshape(8, 128).T),
        "lnb8": c(ln_b[0].reshape(8, 128).T),
    }


def kernel(x_prompt, x_sample, cache_k, cache_v, cache_logf, state_conv, norm_g, w_in, b_f, q_g, k_g,
           w_dw, b_dw, ln_g, ln_b, w_pa, w_pb, w_out):
    NCORE = 8
    x_prompt = np.asarray(x_prompt)
    x_sample = np.asarray(x_sample)
    B, T, _ = x_prompt.shape
    BS = x_sample.shape[0]
    NP = B // NCORE
    NS = BS // NCORE
    nc = _get_nc(NP, NS, T)
    common = _common_maps(*[np.asarray(a) for a in (norm_g, w_in, b_f, q_g, k_g, w_dw, b_dw, ln_g, ln_b, w_pa, w_pb, w_out)])
    ck = np.asarray(cache_k)[0].reshape(BS, PAST, D)
    cv = np.asarray(cache_v)[0].reshape(BS, PAST, D)
    cl = np.asarray(cache_logf)[0]
    sc = np.asarray(state_conv)[0]
    f = np.float32
    in_maps = []
    for i in range(NCORE):
        m = dict(common)
        m["xp"] = np.ascontiguousarray(x_prompt[i * NP:(i + 1) * NP], dtype=f)
        m["xs"] = np.ascontiguousarray(x_sample[i * NS:(i + 1) * NS], dtype=f)
        m["ck"] = np.ascontiguousarray(ck[i * NS:(i + 1) * NS], dtype=f)
        m["cv"] = np.ascontiguousarray(cv[i * NS:(i + 1) * NS], dtype=f)
        m["cl"] = np.ascontiguousarray(cl[i * NS:(i + 1) * NS], dtype=f)
        m["sc"] = np.ascontiguousarray(sc[i * NS:(i + 1) * NS], dtype=f)
        in_maps.append(m)
    res = run_bass_kernel_spmd(nc, in_maps, core_ids=list(range(NCORE)))
    R = res.results
    cat = lambda n: np.concatenate([np.asarray(r[n], dtype=f) for r in R], axis=0)
    y_p = cat("yp")
    y_s = cat("ys")
    k_p = cat("kp").reshape(1, B, T, H, HD)
    v_p = cat("vp").reshape(1, B, T, H, HD)
    f_p = cat("fp").reshape(1, B, T, H)
    c_p = cat("cp").reshape(1, B, HIST, D)
    k_s = cat("ks").reshape(1, BS, SSEQ, H, HD)
    v_s = cat("vs").reshape(1, BS, SSEQ, H, HD)
    f_s = cat("fs").reshape(1, BS, SSEQ, H)
    c_s = cat("cs").reshape(1, BS, HIST, D)
    return (y_p, y_s, k_p, v_p, f_p, c_p, k_s, v_s, f_s, c_s)
```

```python
import numpy as np
from contextlib import ExitStack
import concourse.bass as bass
import concourse.mybir as mybir
from concourse.bass_utils import run_bass_kernel_spmd

F32 = mybir.dt.float32
BF16 = mybir.dt.bfloat16
AF = mybir.ActivationFunctionType
ALU = mybir.AluOpType
AX = mybir.AxisListType.X

D = 1024
H = 16
HD = 64
CONV_K = 31
HIST = 30
EPS = 1e-6
PAST = 1024
SSEQ = 32
OFF_Q = 0
OFF_K = 1024
OFF_V = 2048
OFF_F = 3072
OFF_GA = 3088
OFF_UA = OFF_GA + 1024
OFF_UB = OFF_UA + 1024
OFF_GB = OFF_UB + 1024
OFF_MA = OFF_GB + 1024
OFF_MB = OFF_MA + 1024
IN_W = OFF_MB + 1024

ENGS = ("pe", "act", "dve", "pool", "sp")


class Res:
    __slots__ = ("w", "rs")

    def __init__(self):
        self.w = None
        self.rs = []


class Op:
    __slots__ = ("eng", "fn", "deps", "sig", "sigval", "key", "ninc")

    def __init__(self, eng, fn, key):
        self.eng = eng
        self.fn = fn
        self.deps = ()
        self.sig = False
        self.sigval = 0
        self.key = key
        self.ninc = 1


class Sched:
    def __init__(self):
        self.ops = {e: [] for e in ENGS}
        self.fence = None
        self.fence_seen = {}
        self.last_dma = {}

    def add(self, eng, fn, reads=(), writes=(), key=None, ninc=1):
        op = Op(eng, fn, key)
        op.ninc = ninc
        deps = set()
        for r in reads:
            if r.w is not None:
                deps.add(r.w)
        for w in writes:
            if w.w is not None:
                deps.add(w.w)
            deps.update(w.rs)
        if self.fence is not None and self.fence_seen.get(eng) is not self.fence:
            deps.update(self.fence)
            self.fence_seen[eng] = self.fence
        if key is not None:
            prev = self.last_dma.get(key)
            if prev is not None:
                deps.add(prev)
            self.last_dma[key] = op
            op.sig = True
        if eng == "pe":
            deps = {d for d in deps if not (d.eng == "pe" and d.key is None)}
        for d in deps:
            d.sig = True
        op.deps = deps
        for w in writes:
            w.w = op
            w.rs = []
        for r in reads:
            r.rs.append(op)
        self.ops[eng].append(op)
        return op

    def barrier(self):
        f = set()
        for e in ENGS:
            for op in reversed(self.ops[e]):
                if op.key is None:
                    f.add(op)
                    break
        f.update(self.last_dma.values())
        self.fence = f
        self.fence_seen = {}


def build(NP, NS, TP):
    nc = bass.Bass("TRN2", target_bir_lowering=False)
    S = Sched()
    NTP = TP // 128
    NTMAX = max(NTP, PAST // 128 + 1)
    TKMAX = NTMAX * 128

    def din(name, shape, dt=F32):
        return nc.dram_tensor(name, list(shape), dt, kind="ExternalInput").ap()

    def dout(name, shape, dt=F32):
        return nc.dram_tensor(name, list(shape), dt, kind="ExternalOutput").ap()

    xp = din("xp", [NP, TP, D])
    xs = din("xs", [NS, SSEQ, D])
    ck = din("ck", [NS, PAST, D])
    cv = din("cv", [NS, PAST, D])
    cl = din("cl", [NS, PAST, H])
    sc = din("sc", [NS, HIST, D])
    w_in = din("w_in", [D, IN_W])
    w_pa = din("w_pa", [D, D])
    w_pb = din("w_pb", [D, D])
    w_out = din("w_out", [D, D])
    g8_d = din("g8", [128, 8])
    bf_d = din("bf_rep", [128, H])
    gq_d = din("gq_col", [64, 1])
    gk_d = din("gk_rep", [128, 128])
    gqr_d = din("gq_rep", [128, 128])
    wdw_d = din("wdw", [128, 8, CONV_K])
    bdw_d = din("bdw8", [128, 8])
    lng_d = din("lng8", [128, 8])
    lnb_d = din("lnb8", [128, 8])

    yp = dout("yp", [NP, TP, D])
    ys = dout("ys", [NS, SSEQ, D])
    kpo = dout("kp", [NP, TP, D])
    vpo = dout("vp", [NP, TP, D])
    fpo = dout("fp", [NP, TP, H])
    cpo = dout("cp", [NP, HIST, D])
    kso = dout("ks", [NS, SSEQ, D])
    vso = dout("vs", [NS, SSEQ, D])
    fso = dout("fs", [NS, SSEQ, H])
    cso = dout("cs", [NS, HIST, D])

    blocks = {}
    off = 0

    def addblk(name, W, ncols=None):
        nonlocal off
        n = ncols if ncols is not None else 8 * W
        blocks[name] = (off, W, n)
        off += n

    for p in range(8):
        addblk(f"qkv{p}", 384)
    addblk("f", 16)
    for p in range(8):
        addblk(f"ga{p}", 128)
    for c in range(8):
        addblk(f"uab{c}", 256)
    for c in range(8):
        addblk(f"gb{c}", 128)
    for c in range(8):
        addblk(f"pm{c}", 512)
    addblk("wo0", 512)
    addblk("wo1", 512)
    SCR_COLS = off
    scr = nc.dram_tensor("wscr", [128, SCR_COLS], BF16, kind="Internal").ap()

    st = ExitStack()
    with st:
        def sb(name, shape, dt=F32):
            return st.enter_context(nc.sbuf_tensor(name, list(shape), dt))

        ident_f = sb("ident_f", [128, 128])
        ident_b = sb("ident_b", [128, 128], BF16)
        mask_f = sb("mask_f", [128, 128])
        mask_b = sb("mask_b", [128, 128], BF16)
        ones_f = sb("ones_f", [128, 128])
        ones_b = sb("ones_b", [128, 128], BF16)
        nmask_b = sb("nmask_b", [128, 128], BF16)
        g8 = sb("g8s", [128, 8])
        bf_rep = sb("bf_reps", [128, H])
        gk_rep = sb("gk_reps", [128, 128])
        gqs_rep = sb("gqs_reps", [128, 128])
        kscale = sb("kscale", [128, 1])
        wdw = sb("wdws", [128, 8, CONV_K])
        bdw8 = sb("bdw8s", [128, 8])
        lng8 = sb("lng8s", [128, 8])
        lnb8 = sb("lnb8s", [128, 8])
        nlnb8 = sb("nlnb8", [128, 8])
        negh = sb("negh", [128, 8])
        negh512 = sb("negh512", [128, 512])
        hT = sb("hT", [128, 8, TP], BF16)
        og = sb("og", [128, 8, TP], BF16)
        lf = sb("lf", [128, NTMAX, H])
        negC = sb("negC", [128, NTMAX, H])
        Cp = sb("Cp", [128, NTMAX, H, 3], BF16)
        ARENA_W = 33280
        arena = sb("arena", [128, ARENA_W])
        banks = [st.enter_context(nc.psum_tensor(f"bank{i}", [128, 512], F32)) for i in range(8)]
        RB = [Res() for _ in range(8)]
        sems = {e: st.enter_context(nc.semaphore(f"sem_{e}")) for e in ("pe", "act", "dve", "pool")}
        dma_sems = {}

        def dsem(key):
            if key not in dma_sems:
                dma_sems[key] = st.enter_context(nc.semaphore(f"dq_{key}"))
            return dma_sems[key]

        R_const = Res()
        R_scr = Res()
        R_hT = Res()
        R_og = Res()
        R_lf = Res()
        R_negC = Res()
        R_Cp = Res()

        class Carver:
            def __init__(self):
                self.off = 0

            def reset(self):
                self.off = 0

            def get(self, shape, dt=F32):
                n = int(np.prod(shape[1:]))
                words = n if dt == F32 else (n + 1) // 2
                a = arena[:, self.off:self.off + words]
                self.off += words
                assert self.off <= ARENA_W, ("arena overflow", self.off)
                if dt != F32:
                    a = a.bitcast(dt)[:, 0:n]
                if len(shape) == 3:
                    a = a.rearrange("p (a b) -> p a b", a=shape[1], b=shape[2])
                elif len(shape) == 4:
                    a = a.rearrange("p (a b c) -> p a b c", a=shape[1], b=shape[2], c=shape[3])
                return a

        CV = Carver()

        def bbf(i):
            return banks[i][:].bitcast(BF16)

        def A_(out, in_, func, reads, writes, scale=1.0, bias=None):
            def fn(e):
                if bias is None:
                    return e.activation(out=out, in_=in_, func=func, scale=scale)
                return e.activation(out=out, in_=in_, func=func, scale=scale, bias=bias)
            return S.add("act", fn, reads, writes)

        def TT(eng, out, in0, in1, op, reads, writes):
            return S.add(eng, lambda e: e.tensor_tensor(out=out, in0=in0, in1=in1, op=op), reads, writes)

        def TS(eng, out, in0, s1, op0, reads, writes, s2=None, op1=None):
            if op1 is None:
                return S.add(eng, lambda e: e.tensor_scalar(out=out, in0=in0, scalar1=s1, scalar2=None, op0=op0), reads, writes)
            return S.add(eng, lambda e: e.tensor_scalar(out=out, in0=in0, scalar1=s1, scalar2=s2, op0=op0, op1=op1), reads, writes)

        def STT(out, in0, scalar, in1, op0, op1, reads, writes):
            return S.add("dve", lambda e: e.scalar_tensor_tensor(out=out, in0=in0, scalar=scalar, in1=in1, op0=op0, op1=op1), reads, writes)

        def CP(eng, out, in_, reads, writes):
            return S.add(eng, lambda e: e.tensor_copy(out=out, in_=in_), reads, writes)

        def MS(eng, ap, val, writes):
            return S.add(eng, lambda e: e.memset(ap, val), (), writes)

        def DMA(out, in_, reads, writes, key, slow=False, eng="sp"):
            def fn(q):
                if slow:
                    return q.dma_start(out=out, in_=in_, allow_slow_non_contiguous=True)
                return q.dma_start(out=out, in_=in_)
            return S.add(eng, fn, reads, writes, key=key)

        def DMA2(pairs, reads, writes, key, eng="sp"):
            def fn(q):
                return [q.dma_start(out=o, in_=i) for (o, i) in pairs]
            return S.add(eng, fn, reads, writes, key=key, ninc=len(pairs))

        PH = {"cur": "pro", "log": [], "n": 0}

        def phase(name):
            PH["log"].append((PH["cur"], PH["n"]))
            PH["cur"] = name

        def MM(groups, reads, writes):
            PH["n"] += len(groups)
            def fn(e):
                ins = None
                for (o, l, r, s0, s1) in groups:
                    ins = e.matmul(o, lhsT=l, rhs=r, start=s0, stop=s1)
                return ins
            return S.add("pe", fn, reads, writes)

        def TR(items, reads, writes):
            PH["n"] += len(items)

            def fn(e):
                ins = None
                for (o, i) in items:
                    ins = e.transpose(o, i, ident_b[:])
                return ins
            return S.add("pe", fn, reads, writes)

        def sigmoid_chain(buf, src, Rbuf, Rsrc, scale=-1.0, bias=None):
            A_(buf, src, AF.Exp, [Rsrc], [Rbuf], scale=scale, bias=bias)
            A_(buf, buf, AF.Ln, [Rbuf], [Rbuf], bias=1.0)
            A_(buf, buf, AF.Exp, [Rbuf], [Rbuf], scale=-1.0)

        MS("pool", ident_f[:], 1.0, [R_const])
        S.add("pool", lambda e: e.affine_select(out=ident_f[:], in_=ident_f[:], pattern=[[-1, 128]], compare_op=ALU.is_equal,
                                                fill=0.0, base=0, channel_multiplier=1), [R_const], [R_const])
        CP("pool", ident_b[:], ident_f[:], [R_const], [R_const])
        MS("pool", mask_f[:], 1.0, [R_const])
        S.add("pool", lambda e: e.affine_select(out=mask_f[:], in_=mask_f[:], pattern=[[1, 128]], compare_op=ALU.is_ge,
                                                fill=0.0, base=0, channel_multiplier=-1), [R_const], [R_const])
        CP("pool", mask_b[:], mask_f[:], [R_const], [R_const])
        MS("pool", ones_f[:], 1.0, [R_const])
        TS("pool", nmask_b[:], mask_f[:], 30000.0, ALU.mult, [R_const], [R_const], s2=-30000.0, op1=ALU.add)
        MS("pool", ones_b[:], 1.0, [R_const])
        MS("pool", negh[:], -0.5, [R_const])
        MS("pool", negh512[:], -0.5, [R_const])
        MS("pool", kscale[:], 1.0, [R_const])
        for (dst, src) in ((g8, g8_d), (bf_rep, bf_d), (gk_rep, gk_d), (bdw8, bdw_d), (lng8, lng_d), (lnb8, lnb_d)):
            DMA(dst[:], src[:, :], [], [R_const], "misc")
        DMA(wdw[:], wdw_d[:, :, :], [], [R_const], "misc")
        DMA(gqs_rep[:], gqr_d[:, :], [], [R_const], "misc")
        TS("pool", gqs_rep[:], gqs_rep[:], float(HD ** -0.5), ALU.mult, [R_const], [R_const])
        gq_t = sb("gq_t", [64, 1])
        DMA(gq_t[:], gq_d[:, :], [], [R_const], "misc")
        TS("pool", kscale[0:64, :], gq_t[:], float(HD ** -0.5), ALU.mult, [R_const], [R_const])
        TS("pool", nlnb8[:], lnb8[:], -1.0, ALU.mult, [R_const], [R_const])

        CV.reset()
        NSF = 4
        stg_f = [CV.get([128, 8, 512]) for _ in range(NSF)]
        stg_b = [CV.get([128, 4096], BF16) for _ in range(3)]
        R_sf = [Res() for _ in range(NSF)]
        R_sbb = [Res(), Res(), Res()]
        segcnt = [0]
        blkcnt = [0]

        def wsrc(m):
            return m.rearrange("(kc p) n -> p kc n", p=128)

        def prep_block(name, segs):
            o_, W, n = blocks[name]
            bs = blkcnt[0] % 3
            blkcnt[0] += 1
            o = 0
            for (src, c0, w, scaled) in segs:
                fs = segcnt[0] % NSF
                eng = "dve" if segcnt[0] % 3 != 2 else "pool"
                segcnt[0] += 1
                DMA(stg_f[fs][:, :, 0:w], wsrc(src)[:, :, c0:c0 + w], [], [R_sf[fs]], ("wB%d" % fs) if fs < 3 else "x0")
                outv = stg_b[bs][:, 0:8 * W].rearrange("p (a b) -> p a b", a=8, b=W)[:, :, o:o + w]
                if scaled:
                    TT(eng, outv, stg_f[fs][:, :, 0:w], g8[:].unsqueeze(2).broadcast_to([128, 8, w]), ALU.mult,
                       [R_sf[fs], R_const], [R_sbb[bs]])
                else:
                    CP(eng, outv, stg_f[fs][:, :, 0:w], [R_sf[fs]], [R_sbb[bs]])
                o += w
            DMA(scr[:, o_:o_ + n], stg_b[bs][:, 0:n], [R_sbb[bs]], [], ("wA%d" % bs) if bs < 2 else "x1", eng="act")

        for p in range(8):
            prep_block(f"qkv{p}", [(w_in, OFF_Q + 128 * p, 128, True), (w_in, OFF_K + 128 * p, 128, True), (w_in, OFF_V + 128 * p, 128, True)])
        prep_block("f", [(w_in, OFF_F, 16, True)])
        for p in range(8):
            prep_block(f"ga{p}", [(w_in, OFF_GA + 128 * p, 128, True)])
        for c in range(8):
            prep_block(f"uab{c}", [(w_in, OFF_UA + 128 * c, 128, True), (w_in, OFF_UB + 128 * c, 128, True)])
        for c in range(8):
            prep_block(f"gb{c}", [(w_in, OFF_GB + 128 * c, 128, True)])
        for c in range(8):
            prep_block(f"pm{c}", [(w_pa, 128 * c, 128, False), (w_pb, 128 * c, 128, False),
                                  (w_in, OFF_MA + 128 * c, 128, True), (w_in, OFF_MB + 128 * c, 128, True)])
        prep_block("wo0", [(w_out, 0, 512, False)])
        prep_block("wo1", [(w_out, 512, 512, False)])
        S.barrier()

        class Ring:
            def __init__(self, name, n, words):
                self.name = name
                self.n = n
                self.words = words
                self.bufs = None
                self.res = [Res() for _ in range(n)]
                self.cnt = 0
                self.plan = []
                self.issued = 0
                self.taken = 0
                self.slots = {}

            def carve(self):
                self.bufs = [CV.get([128, self.words], BF16) for _ in range(self.n)]

            def set_plan(self, names):
                self.plan = list(names)
                self.issued = 0
                self.taken = 0

            def _issue(self):
                i = self.issued
                o_, W, n = blocks[self.plan[i]]
                s = self.cnt % self.n
                self.cnt += 1
                DMA(self.bufs[s][:, 0:n], scr[:, o_:o_ + n], [], [self.res[s]], "misc" if self.name == "wF" else f"{self.name}{s}")
                self.slots[i] = (self.bufs[s][:, 0:n], self.res[s])
                self.issued += 1

            def next(self, k=1):
                i = self.taken
                while self.issued < min(len(self.plan), i + self.n):
                    self._issue()
                self.taken += k
                if k == 1:
                    return self.slots.pop(i)
                return [self.slots.pop(i + q) for q in range(k)]

        def w3(ap, a, b):
            return ap.rearrange("p (a b) -> p a b", a=a, b=b)

        import os as _os
        _kseq2 = _os.environ.get("KSEQ2", "")
        seqno = [-1]

        def run_sequence(kind, b):
            seqno[0] += 1
            isP = kind == "p"
            NTn = NTP if isP else 1
            NTp = 0 if isP else PAST // 128
            NT = NTn + NTp
            CW = 512 if isP else 128
            NCH = NTn * 128 // CW
            VR = 128 if isP else SSEQ
            xsrc = xp[b] if isP else xs[b]
            y_o = yp[b] if isP else ys[b]
            k_o = kpo[b] if isP else kso[b]
            v_o = vpo[b] if isP else vso[b]
            f_o = fpo[b] if isP else fso[b]
            c_o = cpo[b] if isP else cso[b]

            CV.reset()
            NXT = 4
            xt = [CV.get([128, D]) for _ in range(NXT)]
            R_xt = [Res() for _ in range(NXT)]
            sq1 = CV.get([128, D])
            R_sq1 = Res()
            ss1 = [CV.get([128, 2]) for _ in range(2)]
            R_ss1 = [Res(), Res()]
            xb = [CV.get([128, D], BF16) for _ in range(2)]
            R_xb = [Res(), Res()]
            wA = Ring("wA", 2, 8 * 384)
            wA.carve()
            wG = Ring("wG", 2, 8 * 128)
            wG.carve()
            wF = Ring("wF", 1, 8 * 16)
            wF.carve()
            wA.set_plan([f"qkv{p}" for p in range(8)])
            wG.set_plan([f"ga{p}" for p in range(8)])
            wF.set_plan(["f"])
            QKT = CV.get([128, 4, TKMAX], BF16)
            QTa = QKT[:, 0:2, :]
            KTa = QKT[:, 2:4, :]
            va = CV.get([128, NTMAX, 192], BF16)
            R_QT, R_KT, R_va = Res(), Res(), Res()
            va4 = va.rearrange("p t (a b) -> p t a b", a=3, b=64)
            NR = 4
            PPB = (0, 1, 7, 2)
            TRB = (4, 5, 6, 3)
            sq2 = [CV.get([128, 256]) for _ in range(NR)]
            ss2 = [CV.get([128, 4]) for _ in range(NR)]
            rs2 = [CV.get([128, 4]) for _ in range(NR)]
            kt = [CV.get([128, 128]) for _ in range(NR)]
            KVB = 4 if isP else 1
            NKV = 3
            koutB = [CV.get([128, KVB, 128]) for _ in range(NKV)]
            voutB = [CV.get([128, KVB, 128]) for _ in range(NKV)]
            R_koutB = [Res() for _ in range(NKV)]
            R_voutB = [Res() for _ in range(NKV)]
            kvb_cnt = [0]
            qa = [CV.get([128, 2, 68], BF16) for _ in range(NR)]
            ka = [CV.get([128, 2, 68], BF16) for _ in range(NR)]
            if not isP:
                kinA = CV.get([128, NTp, 128])
                vinA = CV.get([128, NTp, 128])
            R_kinA, R_vinA = Res(), Res()
            R_sq2 = [Res() for _ in range(NR)]
            R_ss2 = [Res() for _ in range(NR)]
            R_rs2 = [Res() for _ in range(NR)]
            R_kt = [Res() for _ in range(NR)]
            R_qa = [Res() for _ in range(NR)]
            R_qaC = [Res() for _ in range(NR)]
            R_ka = [Res() for _ in range(NR)]
            pfa = CV.get([128, 16 + NTMAX, H])
            pfb = CV.get([128, 16 + NTMAX, H])
            R_pfa, R_pfb = Res(), Res()
            NPR = 4
            Pb = [CV.get([128, 512], BF16) for _ in range(NPR)]
            R_P = [Res() for _ in range(NPR)]
            gt = [CV.get([128, 512]) for _ in range(2)]
            sg = [CV.get([128, 512]) for _ in range(2)]
            ldb = CV.get([128, 512])
            t2 = CV.get([128, 512])
            R_gt = [Res(), Res()]
            R_sg = [Res(), Res()]
            R_ld, R_t2 = Res(), Res()
            zz = CV.get([128, NTMAX * H])
            r1 = CV.get([128, NTMAX * H])
            R_zz, R_r1 = Res(), Res()

            MS("pool", va[:, :, 64:128], 1.0, [R_va])
            for s_ in range(NR):
                MS("pool", ka[s_][:, :, 64:68], 1.0, [R_ka[s_]])
            MS("pool", pfa[:, 0:16, :], 0.0, [R_pfa])
            MS("pool", pfb[:, 0:16, :], 0.0, [R_pfb])

            phase(kind + "A1")
            def a1_load(t):
                x_ = t % NXT
                if not isP:
                    MS("pool", xt[x_][:], 0.0, [R_xt[x_]])
                DMA(xt[x_][0:VR, :], xsrc[t * 128:t * 128 + VR, :], [], [R_xt[x_]], f"x{x_}")

            def a1_s1(t):
                s_ = t % 2
                x_ = t % NXT
                A_(sq1[:, :], xt[x_][:, :], AF.Square, [R_xt[x_]], [R_sq1])
                S.add("dve", (lambda o, i: (lambda e: e.tensor_reduce(out=o, in_=i, axis=AX, op=ALU.add)))(ss1[s_][:, 0:1], sq1[:, :]),
                      [R_sq1], [R_ss1[s_]])
                TS("pool", ss1[s_][:, 0:1], ss1[s_][:, 0:1], 1.0 / D, ALU.mult, [R_ss1[s_]], [R_ss1[s_]], s2=EPS, op1=ALU.add)
                TT("pool", ss1[s_][:, 1:2], ss1[s_][:, 0:1], negh[:, 0:1], ALU.pow, [R_ss1[s_], R_const], [R_ss1[s_]])

            def a1_s2(t):
                s_ = t % 2
                x_ = t % NXT
                TS("dve", xb[s_][:, :], xt[x_][:, :], ss1[s_][:, 1:2], ALU.mult, [R_xt[x_], R_ss1[s_]], [R_xb[s_]])
                if t + NXT < NTn:
                    a1_load(t + NXT)
                bk = 4 + s_
                trv = w3(bbf(bk), 8, 128)
                TR([(trv[:, kc, :], xb[s_][:, kc * 128:(kc + 1) * 128]) for kc in range(8)], [R_xb[s_], R_const], [RB[bk]])

            def a1_s3(t):
                s_ = t % 2
                bk = 4 + s_
                trv = w3(bbf(bk), 8, 128)
                CP("dve", hT[:, :, t * 128:(t + 1) * 128], trv, [RB[bk]], [R_hT])
                MM([(banks[2][:, t * 16:(t + 1) * 16], hT[:, kc, t * 128:(t + 1) * 128], wfv[:, kc, :], kc == 0, kc == 7) for kc in range(8)],
                   [R_hT, R_wf], [RB[2]])

            wf_ap, R_wf = wF.next()
            wfv = w3(wf_ap, 8, 16)
            for t_ in range(min(NXT, NTn)):
                a1_load(t_)
            a1_s1(0)
            for t in range(NTn):
                a1_s2(t)
                if t + 1 < NTn:
                    a1_s1(t + 1)
                a1_s3(t)

            if seqno[0] >= 1 and _kseq2 == "A1":
                S.barrier()
                return
            phase(kind + "A2")
            zv = w3(zz[:, 0:NTn * H], NTn, H)
            TT("dve", zv, w3(banks[2][:, 0:NTn * H], NTn, H), bf_rep[:].unsqueeze(1).broadcast_to([128, NTn, H]), ALU.add,
               [RB[2], R_const], [R_zz])
            A_(zz[:, 0:NTn * H], zz[:, 0:NTn * H], AF.Exp, [R_zz], [R_zz], scale=-1.0)
            A_(zz[:, 0:NTn * H], zz[:, 0:NTn * H], AF.Ln, [R_zz], [R_zz], bias=1.0)
            if not isP:
                for t_ in range(NTp):
                    DMA(lf[:, t_, :], cl[b, t_ * 128:(t_ + 1) * 128, :], [], [R_lf], "misc")
            TS("dve", lf[:, NTp:NT, :], zv, -1.0, ALU.mult, [R_zz], [R_lf])
            if isP:
                for t_ in range(NT):
                    DMA(f_o[t_ * 128:(t_ + 1) * 128, :], lf[:, t_, :], [R_lf], [], "ast", eng="act")
            else:
                DMA(f_o[:, :], lf[0:VR, NTp, :], [R_lf], [], "ast", eng="act")
            PAD = 16
            CP("dve", pfa[:, PAD:PAD + 1, :], lf[:, 0:1, :], [R_lf], [R_pfa])
            if NT > 1:
                TT("dve", pfa[:, PAD + 1:PAD + NT, :], lf[:, 1:NT, :], lf[:, 0:NT - 1, :], ALU.add, [R_lf], [R_pfa])
            cur, Rcur, nxt, Rnxt = pfa, R_pfa, pfb, R_pfb
            sh = 2
            while sh < NT:
                TT("dve", nxt[:, PAD:PAD + NT, :], cur[:, PAD:PAD + NT, :], cur[:, PAD - sh:PAD + NT - sh, :], ALU.add, [Rcur], [Rnxt])
                cur, Rcur, nxt, Rnxt = nxt, Rnxt, cur, Rcur
                sh *= 2
            MM([(banks[3][:, 0:NT * H], ones_f[:], cur[:, PAD - 1:PAD - 1 + NT, :], True, False),
                (banks[3][:, 0:NT * H], mask_f[:], lf[:, 0:NT, :], False, True)], [R_lf, Rcur, R_const], [RB[3]])
            cps = w3(banks[3][:, 0:NT * H], NT, H)
            TS("dve", negC[:, 0:NT, :], cps, -1.0, ALU.mult, [RB[3]], [R_negC])
            r1v = w3(r1[:, 0:NT * H], NT, H)
            CP("dve", Cp[:, 0:NT, :, 0], cps, [RB[3]], [R_Cp])
            TT("dve", r1v, cps, Cp[:, 0:NT, :, 0], ALU.subtract, [RB[3], R_Cp], [R_r1])
            CP("dve", Cp[:, 0:NT, :, 1], r1v, [R_r1], [R_Cp])
            TT("dve", r1v, r1v, Cp[:, 0:NT, :, 1], ALU.subtract, [R_r1, R_Cp], [R_r1])
            CP("dve", Cp[:, 0:NT, :, 2], r1v, [R_r1], [R_Cp])

            if seqno[0] >= 1 and _kseq2 == "A2":
                S.barrier()
                return
            kvcnt = [0]
            for p in range(8):
                wq_ap, R_wq = wA.next()
                wqv = w3(wq_ap, 8, 384)
                wg_ap, R_wg = wG.next()
                wgv = w3(wg_ap, 8, 128)
                if NTp:
                    DMA2([(kinA[:, :, :], ck[b, :, p * 128:(p + 1) * 128].rearrange("(n q) c -> q n c", q=128)),
                          (vinA[:, :, :], cv[b, :, p * 128:(p + 1) * 128].rearrange("(n q) c -> q n c", q=128))], [], [R_kinA, R_vinA], "kvi0")
                for t in range(NTp):
                    s_ = kvcnt[0] % NR
                    kvcnt[0] += 1
                    TT("dve", ka[s_][:, :, 0:64], w3(kinA[:, t, :], 2, 64), w3(gqs_rep[:, :], 2, 64), ALU.mult, [R_kinA, R_const], [R_ka[s_]])
                    CP("pool", va[:, t, 0:64], vinA[:, t, 0:64], [R_vinA], [R_va])
                    CP("pool", va[:, t, 128:192], vinA[:, t, 64:128], [R_vinA], [R_va])
                    bk = TRB[s_]
                    trv = w3(bbf(bk)[:, 0:512], 4, 128)
                    TR([(trv[0:67, 2 + hh, :], ka[s_][:, hh, 0:67]) for hh in range(2)], [R_ka[s_], R_const], [RB[bk]])
                    A_(KTa[0:67, :, t * 128:(t + 1) * 128], trv[0:67, 2:4, :], AF.Copy, [RB[bk]], [R_KT])

                def proj(t):
                    bk = PPB[t % NR]
                    MM([(banks[bk][:, 0:384], hT[:, kc, t * 128:(t + 1) * 128], wqv[:, kc, :], kc == 0, kc == 7) for kc in range(8)],
                       [R_hT, R_wq], [RB[bk]])

                def kvslot(t):
                    return (kvbase[0] + t // KVB) % NKV

                def kout_t(t):
                    return koutB[kvslot(t)][:, t % KVB, :]

                def vout_t(t):
                    return voutB[kvslot(t)][:, t % KVB, :]

                def epi1(t):
                    bk = PPB[t % NR]
                    s_ = t % NR
                    pp = banks[bk]
                    A_(sq2[s_][:, :], pp[:, 0:256], AF.Square, [RB[bk]], [R_sq2[s_]])
                    A_(vout_t(t), pp[:, 256:384], AF.Copy, [RB[bk]], [R_voutB[kvslot(t)]])
                    S.add("dve", (lambda o, i: (lambda e: e.tensor_reduce(out=o, in_=i, axis=AX, op=ALU.add)))(ss2[s_][:, :], w3(sq2[s_][:, :], 4, 64)),
                          [R_sq2[s_]], [R_ss2[s_]])
                    A_(rs2[s_][:, :], ss2[s_][:, :], AF.Ln, [R_ss2[s_]], [R_rs2[s_]], scale=1.0 / HD, bias=EPS)
                    A_(rs2[s_][:, :], rs2[s_][:, :], AF.Exp, [R_rs2[s_]], [R_rs2[s_]], scale=-0.5)

                def epi2(t):
                    bk = PPB[t % NR]
                    s_ = t % NR
                    T = NTp + t
                    pp = banks[bk]
                    TT("dve", qa[s_][:, :, 0:64], w3(pp[:, 0:128], 2, 64), rs2[s_][:, 0:2].unsqueeze(2).broadcast_to([128, 2, 64]), ALU.mult,
                       [RB[bk], R_rs2[s_]], [R_qa[s_]])
                    CP("pool", qa[s_][:, :, 64:67], Cp[:, T, 2 * p:2 * p + 2, :], [R_Cp], [R_qaC[s_]])
                    TT("dve", w3(kt[s_][:, :], 2, 64), w3(pp[:, 128:256], 2, 64), rs2[s_][:, 2:4].unsqueeze(2).broadcast_to([128, 2, 64]), ALU.mult,
                       [RB[bk], R_rs2[s_]], [R_kt[s_]])
                    ks_ = kvslot(t)
                    TT("dve", kout_t(t), kt[s_][:, :], gk_rep[:, :], ALU.mult, [R_kt[s_], R_const], [R_koutB[ks_]])
                    TT("dve", ka[s_][:, :, 0:64], w3(kout_t(t), 2, 64), w3(gqs_rep[:, :], 2, 64), ALU.mult, [R_koutB[ks_], R_const], [R_ka[s_]])
                    CP("pool", va4[:, T, 0:3:2, :], w3(vout_t(t), 2, 64), [R_voutB[ks_]], [R_va])
                    if t % KVB == KVB - 1:
                        t0_ = t - (KVB - 1)
                        if isP:
                            kdst = k_o[t0_ * 128:(t + 1) * 128, p * 128:(p + 1) * 128].rearrange("(n q) c -> q n c", q=128)
                            vdst = v_o[t0_ * 128:(t + 1) * 128, p * 128:(p + 1) * 128].rearrange("(n q) c -> q n c", q=128)
                            DMA2([(kdst, koutB[ks_][:, :, :]), (vdst, voutB[ks_][:, :, :])], [R_koutB[ks_], R_voutB[ks_]], [], f"kvo{ks_}", eng="act")
                        else:
                            DMA2([(k_o[0:VR, p * 128:(p + 1) * 128], koutB[ks_][0:VR, 0, :]),
                                  (v_o[0:VR, p * 128:(p + 1) * 128], voutB[ks_][0:VR, 0, :])], [R_koutB[ks_], R_voutB[ks_]], [], f"kvo{ks_}", eng="act")

                def trans(t):
                    s_ = t % NR
                    T = NTp + t
                    bk = TRB[s_]
                    trv = w3(bbf(bk)[:, 0:512], 4, 128)
                    items = [(trv[0:67, hh, :], qa[s_][:, hh, 0:67]) for hh in range(2)]
                    items += [(trv[0:67, 2 + hh, :], ka[s_][:, hh, 0:67]) for hh in range(2)]
                    TR(items, [R_qa[s_], R_qaC[s_], R_ka[s_], R_const], [RB[bk]])
                    A_(QKT[0:67, :, T * 128:(T + 1) * 128], trv[0:67, 0:4, :], AF.Copy, [RB[bk]], [R_QT, R_KT])

                phase(kind + "A3")
                kvbase = [kvb_cnt[0]]
                kvb_cnt[0] += (NTn + KVB - 1) // KVB
                for t in range(min(3, NTn)):
                    proj(t)
                for t in range(min(2, NTn)):
                    epi1(t)
                epi2(0)
                for t in range(NTn):
                    if t + 3 < NTn:
                        proj(t + 3)
                    if t + 2 < NTn:
                        epi1(t + 2)
                    if t + 1 < NTn:
                        epi2(t + 1)
                    trans(t)

                phase(kind + "A4")
                SB = (7, 0, 1)

                fillers = []

                def ga_ops(G, now=False):
                    cols = slice(G * CW, (G + 1) * CW)
                    g_ = G % 2
                    ops_ = []
                    for kc in range(8):
                        ops_.append((lambda kc=kc: MM([(banks[6][:, 0:CW], wgv[:, kc, :], hT[:, kc, cols], kc == 0, kc == 7)], [R_hT, R_wg], [RB[6]])))
                    ops_.append(lambda: A_(gt[g_][:, 0:CW], banks[6][:, 0:CW], AF.Exp, [RB[6]], [R_gt[g_]], scale=-1.0))
                    ops_.append(lambda: A_(gt[g_][:, 0:CW], gt[g_][:, 0:CW], AF.Ln, [R_gt[g_]], [R_gt[g_]], bias=1.0))
                    ops_.append(lambda: A_(gt[g_][:, 0:CW], gt[g_][:, 0:CW], AF.Exp, [R_gt[g_]], [R_gt[g_]], scale=-1.0))
                    ops_.append(lambda: TT("dve", sg[g_][:, 0:CW], banks[6][:, 0:CW], gt[g_][:, 0:CW], ALU.mult, [RB[6], R_gt[g_]], [R_sg[g_]]))
                    if now:
                        for f_ in ops_:
                            f_()
                    else:
                        fillers.extend(ops_)

                def norm_ops(G, now=False):
                    cols = slice(G * CW, (G + 1) * CW)
                    g_ = G % 2
                    oa, ob_ = (2, 3) if G % 2 == 0 else (4, 5)
                    ops_ = [
                        lambda: S.add("dve", lambda e: e.reciprocal(out=ldb[0:64, 0:CW], in_=banks[oa][64:128, 0:CW]), [RB[oa]], [R_ld]),
                        lambda: S.add("dve", lambda e: e.reciprocal(out=ldb[64:128, 0:CW], in_=banks[ob_][0:64, 0:CW]), [RB[ob_]], [R_ld]),
                        lambda: TT("dve", t2[:, 0:CW], sg[g_][:, 0:CW], ldb[:, 0:CW], ALU.mult, [R_sg[g_], R_ld], [R_t2]),
                        lambda: TT("dve", og[0:64, p, cols], banks[oa][0:64, 0:CW], t2[0:64, 0:CW], ALU.mult, [RB[oa], R_t2], [R_og]),
                        lambda: TT("dve", og[64:128, p, cols], banks[ob_][64:128, 0:CW], t2[64:128, 0:CW], ALU.mult, [RB[ob_], R_t2], [R_og]),
                    ]
                    if now:
                        for f_ in ops_:
                            f_()
                    else:
                        fillers.extend(ops_)

                blks = []
                for G in range(NCH):
                    first_diag = NTp + G * CW // 128
                    nkb = first_diag + CW // 128
                    for hh in range(2):
                        for j in range(nkb):
                            blks.append(dict(G=G, hh=hh, j=j, c0=max(0, j - first_diag) * 128, diag=j >= first_diag,
                                             ob=((2, 3) if G % 2 == 0 else (4, 5))[hh], hoff=0 if hh == 0 else 64,
                                             head=2 * p + hh, first=j == 0, last=j == nkb - 1, qbase=NTp * 128 + G * CW))

                def qk(i):
                    bl = blks[i]
                    sbk = SB[i % 3]
                    c0 = bl["c0"]
                    j = bl["j"]
                    hh = bl["hh"]
                    kT = KTa[0:67, hh, j * 128:(j + 1) * 128]
                    qb = bl["qbase"]
                    if bl["diag"]:
                        grp = [(banks[sbk][:, c0:c0 + 128], ident_b[:, :], nmask_b[:, :], True, False),
                               (banks[sbk][:, c0:c0 + 128], kT, QTa[0:67, hh, qb + c0:qb + c0 + 128], False, True)]
                        if c0 + 128 < CW:
                            grp.append((banks[sbk][:, c0 + 128:CW], kT, QTa[0:67, hh, qb + c0 + 128:qb + CW], True, True))
                        MM(grp, [R_KT, R_QT, R_const], [RB[sbk]])
                    else:
                        MM([(banks[sbk][:, c0:CW], kT, QTa[0:67, hh, qb + c0:qb + CW], True, True)], [R_KT, R_QT], [RB[sbk]])

                ga_ops(0)
                nb = len(blks)
                qk(0)
                if nb > 1:
                    qk(1)
                for i in range(nb):
                    bl = blks[i]
                    if i + 2 < nb:
                        qk(i + 2)
                    c0 = bl["c0"]
                    j = bl["j"]
                    sbk = SB[i % 3]
                    ps_ = i % NPR
                    A_(Pb[ps_][:, c0:CW], banks[sbk][:, c0:CW], AF.Exp, [RB[sbk], R_negC], [R_P[ps_]], bias=negC[:, j, bl["head"]:bl["head"] + 1])
                    MM([(banks[bl["ob"]][:, c0:CW], va[:, j, bl["hoff"]:bl["hoff"] + 128], Pb[ps_][:, c0:CW], bl["first"], bl["last"])],
                       [R_va, R_P[ps_]], [RB[bl["ob"]]])
                    if bl["hh"] == 0 and j == 1 and bl["G"] + 1 < NCH:
                        ga_ops(bl["G"] + 1)
                    if bl["hh"] == 1 and bl["last"]:
                        norm_ops(bl["G"])
                    if fillers:
                        fillers.pop(0)()
                    if len(fillers) > 10:
                        fillers.pop(0)()
                while fillers:
                    fillers.pop(0)()

            S.barrier()
            if seqno[0] >= 1 and _kseq2 == "A":
                return
            CV.reset()
            wB = Ring("wB", 3, 4096)
            wB.carve()
            plan = []
            plan += [f"uab{c}" for c in range(8)]
            for _g in range(NCH):
                for c in range(8):
                    plan.append(f"gb{c}")
                    if _g + 1 < NCH:
                        plan.append(f"uab{c}")
                plan += [f"pm{c}" for c in range(8)] + ["wo0", "wo1"]
            wB.set_plan(plan)
            dgb = [CV.get([128, CONV_K, 128], BF16) for _ in range(2)]
            R_dgb = [Res(), Res()]
            u = CV.get([128, 8, HIST + 512], BF16)
            ufp = CV.get([128, 8, 32])
            conv = CV.get([128, 8, 512])
            cbf = [CV.get([128, 512], BF16) for _ in range(2)]
            csq = [CV.get([128, 512], BF16) for _ in range(2)]
            mean = CV.get([128, 512])
            msq = CV.get([128, 512])
            rstd = CV.get([128, 512])
            Bm = CV.get([128, 512])
            tb = [CV.get([128, 512]) for _ in range(2)]
            tb2 = [CV.get([128, 512]) for _ in range(2)]
            tq = CV.get([128, 512])
            v0 = CV.get([128, 512])
            sv = CV.get([128, 512])
            t3 = CV.get([128, 512])
            zb = CV.get([128, 8, 512], BF16)
            mT = CV.get([128, 8, 512], BF16)
            NXR = 4
            xr = [CV.get([128, D]) for _ in range(NXR)]
            tailT = CV.get([128, D])
            R_tail = Res()
            R_u, R_ufp, R_conv = Res(), Res(), Res()
            R_cbf = [Res(), Res()]
            R_csq = [Res(), Res()]
            R_mean, R_msq, R_rstd, R_Bm = Res(), Res(), Res(), Res()
            R_tb = [Res(), Res()]
            R_tb2 = [Res(), Res()]
            R_tq, R_v0, R_sv, R_t3, R_zb, R_m = Res(), Res(), Res(), Res(), Res(), Res()
            R_xr = [Res() for _ in range(NXR)]
            prebuilt = [False]
            tcnt = [0]
            xcnt = [0]

            def hist_for(G):
                if isP:
                    if G == 0:
                        MS("pool", u[:, :, 0:HIST], 0.0, [R_u])
                    else:
                        CP("pool", u[:, :, 0:HIST], u[:, :, CW:CW + HIST], [R_u], [R_u])
                else:
                    DMA(tailT[0:HIST, :], sc[b][:, :], [], [R_tail], "misc")
                    hv = banks[6][:, 0:256].rearrange("p (a b) -> p a b", a=8, b=32)
                    S.add("pe", (lambda hv_: (lambda e: [e.transpose(hv_[:, c_, 0:HIST], tailT[0:HIST, c_ * 128:(c_ + 1) * 128], ident_f[0:HIST, 0:HIST]) for c_ in range(8)][-1]))(hv),
                          [R_tail, R_const], [RB[6]])
                    TS("dve", u[:, :, 0:HIST], hv[:, :, 0:HIST], 2.0, ALU.mult, [RB[6]], [R_u])

            def b1_chunk(G, c):
                cols_ = slice(G * CW, (G + 1) * CW)
                last_ = G == NCH - 1
                wb_ap, R_wb = wB.next()
                wv = w3(wb_ap, 8, 256)
                ba = (2 * c) % 4
                bb_ = ba + 1
                MM([(banks[ba][:, 0:CW], wv[:, kc, 0:128], hT[:, kc, cols_], kc == 0, kc == 7) for kc in range(8)], [R_hT, R_wb], [RB[ba]])
                MM([(banks[bb_][:, 0:CW], wv[:, kc, 128:256], hT[:, kc, cols_], kc == 0, kc == 7) for kc in range(8)], [R_hT, R_wb], [RB[bb_]])
                ts_ = tcnt[0] % 2
                tcnt[0] += 1
                A_(tb[ts_][:, 0:CW], banks[bb_][:, 0:CW], AF.Tanh, [RB[bb_]], [R_tb[ts_]], scale=0.5)
                STT(u[:, c, HIST:HIST + CW], tb[ts_][:, 0:CW], 1.0, banks[ba][:, 0:CW], ALU.add, ALU.mult, [RB[ba], R_tb[ts_]], [R_u])
                if last_:
                    o32 = CW - 32 if isP else 0
                    STT(ufp[:, c, :], tb[ts_][:, o32:o32 + 32], 1.0, banks[ba][:, o32:o32 + 32], ALU.add, ALU.mult, [RB[ba], R_tb[ts_]], [R_ufp])

            def b1_tail(G):
                if G == NCH - 1:
                    S.add("pe", lambda e: [e.transpose(banks[2 + c_ // 4][0:32, (c_ % 4) * 128:(c_ % 4 + 1) * 128], ufp[:, c_, :], ident_f[:, :]) for c_ in range(8)][-1],
                          [R_ufp, R_const], [RB[2], RB[3]])
                    A_(tailT[0:32, 0:512], banks[2][0:32, :], AF.Identity, [RB[2]], [R_tail], scale=0.5)
                    A_(tailT[0:32, 512:1024], banks[3][0:32, :], AF.Identity, [RB[3]], [R_tail], scale=0.5)
                    DMA(c_o[:, :], tailT[2:32, :], [R_tail], [], "ast", eng="act")

            for G in range(NCH):
                cols = slice(G * CW, (G + 1) * CW)
                last = G == NCH - 1
                if G == 0:
                    hist_for(0)
                    phase(kind + "B1")
                    for c in range(8):
                        b1_chunk(0, c)
                    b1_tail(0)
                phase(kind + "B2")
                def stats_mm(c):
                    s_ = c % 2
                    MM([(banks[6][:, 0:CW], ones_b[:, :], cbf[s_][:, 0:CW], c == 0, c == 7)], [R_cbf[s_], R_const], [RB[6]])
                    MM([(banks[7][:, 0:CW], ones_b[:, :], csq[s_][:, 0:CW], c == 0, c == 7)], [R_csq[s_], R_const], [RB[7]])

                def mkdiag(c):
                    d_ = c % 2
                    TT("dve", dgb[d_][:, :, :], ident_b[:].unsqueeze(1).broadcast_to([128, CONV_K, 128]),
                       wdw[:, c, :].unsqueeze(2).broadcast_to([128, CONV_K, 128]), ALU.mult, [R_const], [R_dgb[d_]])

                if not prebuilt[0]:
                    mkdiag(0)
                for c in range(8):
                    d_ = c % 2
                    if c + 1 < 8 and not (c == 0 and prebuilt[0]):
                        mkdiag(c + 1)
                    wdv, R_wd = dgb[d_], R_dgb[d_]
                    bk = 4 + c % 2
                    MM([(banks[bk][:, 0:CW], wdv[:, j, :], u[:, c, j:j + CW], j == 0, j == CONV_K - 1) for j in range(CONV_K)], [R_u, R_wd], [RB[bk]])
                    if c > 0:
                        stats_mm(c - 1)
                    s_ = c % 2
                    A_(conv[:, c, 0:CW], banks[bk][:, 0:CW], AF.Identity, [RB[bk], R_const], [R_conv], scale=0.5, bias=bdw8[:, c:c + 1])
                    A_(csq[s_][:, 0:CW], banks[bk][:, 0:CW], AF.Square, [RB[bk], R_const], [R_csq[s_]], scale=0.5, bias=bdw8[:, c:c + 1])
                    CP("dve", cbf[s_][:, 0:CW], conv[:, c, 0:CW], [R_conv], [R_cbf[s_]])
                stats_mm(7)
                TS("dve", mean[:, 0:CW], banks[6][:, 0:CW], 1.0 / D, ALU.mult, [RB[6]], [R_mean])
                TT("dve", msq[:, 0:CW], mean[:, 0:CW], mean[:, 0:CW], ALU.mult, [R_mean], [R_msq])
                STT(msq[:, 0:CW], banks[7][:, 0:CW], 1.0 / D, msq[:, 0:CW], ALU.mult, ALU.subtract, [RB[7], R_msq], [R_msq])
                A_(rstd[:, 0:CW], msq[:, 0:CW], AF.Ln, [R_msq], [R_rstd], bias=EPS)
                A_(rstd[:, 0:CW], rstd[:, 0:CW], AF.Exp, [R_rstd], [R_rstd], scale=-0.5)
                STT(Bm[:, 0:CW], mean[:, 0:CW], -1.0, rstd[:, 0:CW], ALU.mult, ALU.mult, [R_mean, R_rstd], [R_Bm])
                if G + 1 < NCH:
                    hist_for(G + 1)
                phase(kind + "B3")
                for c in range(8):
                    wb_ap, R_wb = wB.next()
                    wv = w3(wb_ap, 8, 128)
                    bk = 6 + c % 2
                    MM([(banks[bk][:, 0:CW], wv[:, kc, :], hT[:, kc, cols], kc == 0, kc == 7) for kc in range(8)], [R_hT, R_wb], [RB[bk]])
                    STT(tq[:, 0:CW], conv[:, c, 0:CW], lng8[:, c:c + 1], rstd[:, 0:CW], ALU.mult, ALU.mult, [R_conv, R_rstd, R_const], [R_tq])
                    STT(v0[:, 0:CW], Bm[:, 0:CW], lng8[:, c:c + 1], tq[:, 0:CW], ALU.mult, ALU.add, [R_Bm, R_tq, R_const], [R_v0])
                    ts_ = tcnt[0] % 2
                    tcnt[0] += 1
                    TS("dve", v0[:, 0:CW], v0[:, 0:CW], lnb8[:, c:c + 1], ALU.add, [R_v0, R_const], [R_v0])
                    A_(tb[ts_][:, 0:CW], v0[:, 0:CW], AF.Tanh, [R_v0], [R_tb[ts_]], scale=0.5)
                    STT(sv[:, 0:CW], tb[ts_][:, 0:CW], 1.0, v0[:, 0:CW], ALU.add, ALU.mult, [R_v0, R_tb[ts_]], [R_sv])
                    A_(tb2[ts_][:, 0:CW], banks[bk][:, 0:CW], AF.Tanh, [RB[bk]], [R_tb2[ts_]], scale=0.5)
                    STT(t3[:, 0:CW], tb2[ts_][:, 0:CW], 1.0, banks[bk][:, 0:CW], ALU.add, ALU.mult, [RB[bk], R_tb2[ts_]], [R_t3])
                    STT(zb[:, c, 0:CW], t3[:, 0:CW], 0.25, sv[:, 0:CW], ALU.mult, ALU.mult, [R_t3, R_sv], [R_zb])
                    if G + 1 < NCH:
                        b1_chunk(G + 1, c)
                if G + 1 < NCH:
                    b1_tail(G + 1)
                phase(kind + "B4")
                for c in range(8):
                    wb_ap, R_wb = wB.next()
                    wv = w3(wb_ap, 8, 512)
                    base = 0 if c % 2 == 0 else 4
                    MM([(banks[base][:, 0:CW], wv[:, kc, 0:128], og[:, kc, cols], kc == 0, kc == 7) for kc in range(8)], [R_og, R_wb], [RB[base]])
                    MM([(banks[base + 1][:, 0:CW], wv[:, kc, 128:256], zb[:, kc, 0:CW], kc == 0, kc == 7) for kc in range(8)], [R_zb, R_wb], [RB[base + 1]])
                    MM([(banks[base + 2][:, 0:CW], wv[:, kc, 256:384], hT[:, kc, cols], kc == 0, kc == 7) for kc in range(8)], [R_hT, R_wb], [RB[base + 2]])
                    MM([(banks[base + 3][:, 0:CW], wv[:, kc, 384:512], hT[:, kc, cols], kc == 0, kc == 7) for kc in range(8)], [R_hT, R_wb], [RB[base + 3]])
                    ts_ = tcnt[0] % 2
                    tcnt[0] += 1
                    A_(tb[ts_][:, 0:CW], banks[base + 2][:, 0:CW], AF.Tanh, [RB[base + 2]], [R_tb[ts_]], scale=0.5)
                    A_(tb2[ts_][:, 0:CW], banks[base + 3][:, 0:CW], AF.Tanh, [RB[base + 3]], [R_tb2[ts_]], scale=0.5)
                    STT(tq[:, 0:CW], tb[ts_][:, 0:CW], 1.0, banks[base][:, 0:CW], ALU.add, ALU.mult, [RB[base], R_tb[ts_]], [R_tq])
                    STT(t3[:, 0:CW], tb2[ts_][:, 0:CW], 1.0, banks[base + 1][:, 0:CW], ALU.add, ALU.mult, [RB[base + 1], R_tb2[ts_]], [R_t3])
                    TT("dve", mT[:, c, 0:CW], tq[:, 0:CW], t3[:, 0:CW], ALU.add, [R_tq, R_t3], [R_m])
                prebuilt[0] = G + 1 < NCH
                phase(kind + "B5")
                wo = [(w3(a_, 8, 512), r_) for (a_, r_) in wB.next(2)]
                for tt in range(CW // 128):
                    xs_ = xcnt[0] % NXR
                    xcnt[0] += 1
                    trow = G * CW + tt * 128
                    DMA(xr[xs_][0:VR, :], xsrc[trow:trow + VR, :], [], [R_xr[xs_]], f"x{xs_}")
                    for hf in range(2):
                        bk = (2 * tt + hf) % 8
                        MM([(banks[bk][:, 0:512], mT[:, kc, tt * 128:(tt + 1) * 128], wo[hf][0][:, kc, :], kc == 0, kc == 7) for kc in range(8)],
                           [R_m, wo[hf][1]], [RB[bk]])
                        STT(xr[xs_][0:VR, hf * 512:(hf + 1) * 512], banks[bk][0:VR, 0:512], 0.5, xr[xs_][0:VR, hf * 512:(hf + 1) * 512], ALU.mult, ALU.add,
                            [RB[bk], R_xr[xs_]], [R_xr[xs_]])
                    DMA(y_o[trow:trow + VR, :], xr[xs_][0:VR, :], [R_xr[xs_]], [], f"yo{xs_ % 2}", eng="act")
                    if prebuilt[0] and tt == 0:
                        mkdiag(0)
                    if prebuilt[0] and tt == 2:
                        mkdiag(1)
            S.barrier()

        import os as _os
        _kstop = _os.environ.get("KSTOP", "")
        for b in range(NS):
            if _kstop not in ("pro", "ponly"):
                run_sequence("s", b)
        for b in range(NP):
            if _kstop not in ("pro", "sonly"):
                run_sequence("p", b)

        cnt = {e: 0 for e in ("pe", "act", "dve", "pool")}
        dcnt = {}
        for e in ENGS:
            for op in S.ops[e]:
                if op.key is not None:
                    dcnt[op.key] = dcnt.get(op.key, 0) + 16 * op.ninc
                    op.sigval = dcnt[op.key]
                    dsem(op.key)
                elif op.sig:
                    cnt[e] += 1
                    op.sigval = cnt[e]

        def emit(eng, h):
            waited = {}
            for op in S.ops[eng]:
                need = {}
                for d in op.deps:
                    k = ("d", d.key) if d.key is not None else ("e", d.eng)
                    if d.sigval > need.get(k, 0):
                        need[k] = d.sigval
                for k, v in need.items():
                    if waited.get(k, 0) < v:
                        sem = dma_sems[k[1]] if k[0] == "d" else sems[k[1]]
                        h.wait_ge(sem, v)
                        waited[k] = v
                ins = op.fn(h)
                if op.key is not None:
                    for i_ in (ins if isinstance(ins, (list, tuple)) else [ins]):
                        i_.then_inc(dma_sems[op.key], 16)
                elif op.sig:
                    ins.then_inc(sems[eng], 1)
            if eng == "sp":
                for k, v in dcnt.items():
                    h.wait_ge(dma_sems[k], v)

        global LAST_STATS
        LAST_STATS = {e: len(S.ops[e]) for e in ENGS}
        LAST_STATS["sems"] = len(dma_sems) + 4
        phase("end")
        LAST_STATS["phases"] = list(PH["log"])
        with nc.Block() as block:
            @block.tensor
            def _(t):
                emit("pe", t)

            @block.scalar
            def _(a):
                emit("act", a)

            @block.vector
            def _(v):
                emit("dve", v)

            @block.gpsimd
            def _(g):
                emit("pool", g)

            @block.sync
            def _(s):
                emit("sp", s)
    return nc


_NC_CACHE = {}
LAST_STATS = {}


def _get_nc(NP, NS, TP):
    k = (NP, NS, TP)
    if k not in _NC_CACHE:
        _NC_CACHE[k] = build(NP, NS, TP)
    return _NC_CACHE[k]


def _common_maps(norm_g, w_in, b_f, q_g, k_g, w_dw, b_dw, ln_g, ln_b, w_pa, w_pb, w_out):
    f = np.float32
    c = lambda a: np.ascontiguousarray(a, dtype=f)
    return {
        "w_in": c(w_in[0]), "w_pa": c(w_pa[0]), "w_pb": c(w_pb[0]), "w_out": c(w_out[0]),
        "g8": c(norm_g[0].reshape(8, 128).T),
        "bf_rep": c(np.tile(b_f[0][None, :], (128, 1))),
        "gq_col": c(q_g[0].reshape(64, 1)),
        "gk_rep": c(np.tile(k_g[0][None, :], (128, 2))),
        "gq_rep": c(np.tile(q_g[0][None, :], (128, 2))),
        "wdw": c(w_dw[0].reshape(CONV_K, 8, 128).transpose(2, 1, 0)),
        "bdw8": c(b_dw[0].reshape(8, 128).T),
        "lng8": c(ln_g[0].reshape(8, 128).T),
        "lnb8": c(ln_b[0].reshape(8, 128).T),
    }


def kernel(x_prompt, x_sample, cache_k, cache_v, cache_logf, state_conv, norm_g, w_in, b_f, q_g, k_g,
           w_dw, b_dw, ln_g, ln_b, w_pa, w_pb, w_out):
    NCORE = 8
    x_prompt = np.asarray(x_prompt)
    x_sample = np.asarray(x_sample)
    B, T, _ = x_prompt.shape
    BS = x_sample.shape[0]
    NP = B // NCORE
    NS = BS // NCORE
    nc = _get_nc(NP, NS, T)
    common = _common_maps(*[np.asarray(a) for a in (norm_g, w_in, b_f, q_g, k_g, w_dw, b_dw, ln_g, ln_b, w_pa, w_pb, w_out)])
    ck = np.asarray(cache_k)[0].reshape(BS, PAST, D)
    cv = np.asarray(cache_v)[0].reshape(BS, PAST, D)
    cl = np.asarray(cache_logf)[0]
    sc = np.asarray(state_conv)[0]
    f = np.float32
    in_maps = []
    for i in range(NCORE):
        m = dict(common)
        m["xp"] = np.ascontiguousarray(x_prompt[i * NP:(i + 1) * NP], dtype=f)
        m["xs"] = np.ascontiguousarray(x_sample[i * NS:(i + 1) * NS], dtype=f)
        m["ck"] = np.ascontiguousarray(ck[i * NS:(i + 1) * NS], dtype=f)
        m["cv"] = np.ascontiguousarray(cv[i * NS:(i + 1) * NS], dtype=f)
        m["cl"] = np.ascontiguousarray(cl[i * NS:(i + 1) * NS], dtype=f)
        m["sc"] = np.ascontiguousarray(sc[i * NS:(i + 1) * NS], dtype=f)
        in_maps.append(m)
    res = run_bass_kernel_spmd(nc, in_maps, core_ids=list(range(NCORE)))
    R = res.results
    cat = lambda n: np.concatenate([np.asarray(r[n], dtype=f) for r in R], axis=0)
    y_p = cat("yp")
    y_s = cat("ys")
    k_p = cat("kp").reshape(1, B, T, H, HD)
    v_p = cat("vp").reshape(1, B, T, H, HD)
    f_p = cat("fp").reshape(1, B, T, H)
    c_p = cat("cp").reshape(1, B, HIST, D)
    k_s = cat("ks").reshape(1, BS, SSEQ, H, HD)
    v_s = cat("vs").reshape(1, BS, SSEQ, H, HD)
    f_s = cat("fs").reshape(1, BS, SSEQ, H)
    c_s = cat("cs").reshape(1, BS, HIST, D)
    return (y_p, y_s, k_p, v_p, f_p, c_p, k_s, v_s, f_s, c_s)
```
